# Optimizing a Trainium2 kernel written in Bass

```python
import math
import jax
import jax.numpy as jnp
from jax import lax

D_MODEL = 1024
BATCH = 16
SEQ = 4096
DEPTH = 1

D_FF = 2816
FFN_RESIDUAL_WEIGHT = 0.5
SSM_D_INNER = 2 * D_MODEL
SSM_HEAD_DIM = 64
SSM_HEADS = SSM_D_INNER // SSM_HEAD_DIM
SSM_GROUPS = 4
SSM_STATE = 128
SSM_CONV = 4
SSM_CHUNK = 128
SSM_CONV_DIM = SSM_D_INNER + 2 * SSM_GROUPS * SSM_STATE
ATTN_Q_HEADS = 16
ATTN_KV_HEADS = 4
ATTN_HEAD_DIM = 64
ATTN_WINDOW = 128
ATTN_BLOCK = 128
ATTN_Q_DIM = ATTN_Q_HEADS * ATTN_HEAD_DIM
ATTN_KV_DIM = ATTN_KV_HEADS * ATTN_HEAD_DIM
REL_BUCKETS = 32
REL_MAX_DISTANCE = 128
RMS_EPS = 1e-6
IN_SPLITS = (D_MODEL, D_MODEL, SSM_D_INNER, SSM_CONV_DIM, SSM_HEADS, ATTN_Q_DIM, ATTN_KV_DIM, ATTN_KV_DIM)
IN_COLS = D_MODEL + D_MODEL + SSM_D_INNER + SSM_CONV_DIM + SSM_HEADS + ATTN_Q_DIM + 2 * ATTN_KV_DIM

kernel_name = 'hybrid_ssd_swa_sink_macaron_block'


def rmsnorm(x, g):
    xf = x.astype(jnp.float32)
    y = xf * lax.rsqrt(jnp.mean(xf * xf, axis=-1, keepdims=True) + RMS_EPS)
    return (y * g.astype(jnp.float32)).astype(x.dtype)


def swiglu(x, w_gate, w_up, w_down):
    return (jax.nn.silu(x @ w_gate) * (x @ w_up)) @ w_down


def t5_causal_bucket(dist):
    max_exact = REL_BUCKETS // 2
    d = jnp.maximum(dist, 1).astype(jnp.float32)
    large = max_exact + (jnp.log(d / max_exact) / math.log(REL_MAX_DISTANCE / max_exact)
                         * (REL_BUCKETS - max_exact)).astype(jnp.int32)
    large = jnp.minimum(large, REL_BUCKETS - 1)
    return jnp.where(dist < max_exact, dist, large)


def causal_depthwise_conv(x, w, bias):
    y = lax.conv_general_dilated(
        x, w[:, None, :].astype(x.dtype), window_strides=(1,), padding=[(SSM_CONV - 1, 0)],
        dimension_numbers=('NWC', 'WIO', 'NWC'), feature_group_count=x.shape[-1])
    return y + bias.astype(x.dtype)


def ssd_chunked_scan(xs, dt, a, bm, cm):
    b, s = xs.shape[:2]
    nc = s // SSM_CHUNK
    q = SSM_CHUNK
    r = SSM_HEADS // SSM_GROUPS
    f32 = jnp.float32
    x = (xs.astype(f32) * dt[..., None]).reshape(b, nc, q, SSM_GROUPS, r, SSM_HEAD_DIM)
    a_cs = jnp.cumsum((dt * a).reshape(b, nc, q, SSM_GROUPS, r), axis=2)
    bc = bm.astype(f32).reshape(b, nc, q, SSM_GROUPS, SSM_STATE)
    cc = cm.astype(f32).reshape(b, nc, q, SSM_GROUPS, SSM_STATE)
    causal = jnp.tril(jnp.ones((q, q), dtype=bool))[:, :, None, None]
    seg = a_cs[:, :, :, None] - a_cs[:, :, None]
    decay = jnp.exp(jnp.where(causal, seg, -jnp.inf))
    scores = jnp.einsum('bcign,bcjgn->bcijg', cc, bc)
    y_diag = jnp.einsum('bcijgr,bcjgrp->bcigrp', scores[..., None] * decay, x)
    x_w = x * jnp.exp(a_cs[:, :, -1:] - a_cs)[..., None]
    states = jnp.einsum('bcjgn,bcjgrp->bcgrpn', bc, x_w)
    chunk_decay = jnp.exp(a_cs[:, :, -1])

    def step(h, inp):
        s_c, d_c = inp
        return h * d_c[..., None, None] + s_c, h

    h0 = jnp.zeros((b, SSM_GROUPS, r, SSM_HEAD_DIM, SSM_STATE), f32)
    _, prev = lax.scan(step, h0, (jnp.moveaxis(states, 1, 0), jnp.moveaxis(chunk_decay, 1, 0)))
    prev = jnp.moveaxis(prev, 0, 1)
    y_off = jnp.einsum('bcign,bcgrpn->bcigrp', cc, prev) * jnp.exp(a_cs)[..., None]
    return (y_diag + y_off).reshape(b, s, SSM_HEADS, SSM_HEAD_DIM)


def ssd_branch(z, xbc, dt_raw, conv_w, conv_b, dt_bias, a_log, d_skip, norm_g):
    b, s = z.shape[:2]
    f32 = jnp.float32
    xbc = jax.nn.silu(causal_depthwise_conv(xbc, conv_w, conv_b))
    xs = xbc[..., :SSM_D_INNER].reshape(b, s, SSM_HEADS, SSM_HEAD_DIM)
    bm = xbc[..., SSM_D_INNER:SSM_D_INNER + SSM_GROUPS * SSM_STATE].reshape(b, s, SSM_GROUPS, SSM_STATE)
    cm = xbc[..., SSM_D_INNER + SSM_GROUPS * SSM_STATE:].reshape(b, s, SSM_GROUPS, SSM_STATE)
    dt = jax.nn.softplus(dt_raw.astype(f32) + dt_bias.astype(f32))
    a = -jnp.exp(a_log.astype(f32))
    y = ssd_chunked_scan(xs, dt, a, bm, cm)
    y = y + d_skip.astype(f32)[:, None] * xs.astype(f32)
    yg = (y.reshape(b, s, SSM_D_INNER) * jax.nn.silu(z.astype(f32))).reshape(b, s, SSM_GROUPS, -1)
    yg = yg * lax.rsqrt(jnp.mean(yg * yg, axis=-1, keepdims=True) + RMS_EPS)
    return (yg.reshape(b, s, SSM_D_INNER) * norm_g.astype(f32)).astype(z.dtype)


def swa_sink_attention(q, k, v, sinks, rel_table):
    b, s = q.shape[:2]
    nb = s // ATTN_BLOCK
    r = ATTN_Q_HEADS // ATTN_KV_HEADS
    f32 = jnp.float32
    blk = ATTN_BLOCK
    qb = q.astype(f32).reshape(b, nb, blk, ATTN_KV_HEADS, r, ATTN_HEAD_DIM) * (ATTN_HEAD_DIM ** -0.5)
    kb = k.astype(f32).reshape(b, nb, blk, ATTN_KV_HEADS, ATTN_HEAD_DIM)
    vb = v.astype(f32).reshape(b, nb, blk, ATTN_KV_HEADS, ATTN_HEAD_DIM)

    def band(t):
        prev = jnp.concatenate([jnp.zeros_like(t[:, :1]), t[:, :-1]], axis=1)
        return jnp.concatenate([prev, t], axis=2)

    kk, vv = band(kb), band(vb)
    qi = jnp.arange(blk)[:, None]
    kj = jnp.arange(2 * blk)[None, :]
    dist = qi + blk - kj
    in_window = (dist >= 0) & (dist < ATTN_WINDOW)
    key_exists = (jnp.arange(nb)[:, None, None] > 0) | (kj >= blk)[None]
    mask = in_window[None] & key_exists
    bias = rel_table.astype(f32)[t5_causal_bucket(jnp.maximum(dist, 0))]
    bias = jnp.transpose(bias, (2, 0, 1)).reshape(ATTN_KV_HEADS, r, 1, blk, 2 * blk)
    logits = jnp.einsum('bnikrd,bnjkd->bkrnij', qb, kk) + bias
    logits = jnp.where(mask, logits, -jnp.inf)
    sink = sinks.astype(f32).reshape(ATTN_KV_HEADS, r, 1, 1)
    m = jnp.maximum(logits.max(axis=-1), sink)
    p = jnp.exp(logits - m[..., None])
    p = p / (p.sum(axis=-1) + jnp.exp(sink - m))[..., None]
    o = jnp.einsum('bkrnij,bnjkd->bnikrd', p, vv)
    return o.reshape(b, s, ATTN_Q_DIM).astype(q.dtype)


def hybrid_mixer(u, w_in, conv_w, conv_b, dt_bias, a_log, d_skip, ssm_norm_g, w_ssm_proj,
                 attn_sinks, rel_table, w_attn_proj, w_out):
    proj = u @ w_in
    parts = []
    start = 0
    for size in IN_SPLITS:
        parts.append(proj[..., start:start + size])
        start += size
    g_ssm, g_attn, z, xbc, dt_raw, q, k, v = parts
    y_ssm = ssd_branch(z, xbc, dt_raw, conv_w, conv_b, dt_bias, a_log, d_skip, ssm_norm_g) @ w_ssm_proj
    y_attn = swa_sink_attention(q, k, v, attn_sinks, rel_table) @ w_attn_proj
    merged = jax.nn.sigmoid(g_ssm) * y_ssm + jax.nn.sigmoid(g_attn) * y_attn
    return merged @ w_out


def setup_inputs(seed: int = 0) -> dict:
    key = jax.random.key(seed)
    ks = jax.random.split(key, 32)
    f32 = jnp.float32

    def dense(k, fan_in, fan_out):
        return jax.random.normal(k, (DEPTH, fan_in, fan_out), f32) * fan_in ** -0.5

    def gain(k, n):
        return 1.0 + 0.05 * jax.random.normal(k, (DEPTH, n), f32)

    dt0 = jnp.exp(jax.random.uniform(ks[20], (DEPTH, SSM_HEADS), f32, math.log(1e-3), math.log(1e-1)))
    return {
        'x': jax.random.normal(ks[0], (BATCH, SEQ, D_MODEL), f32),
        'ffn1_pre_g': gain(ks[1], D_MODEL),
        'ffn1_w_gate': dense(ks[2], D_MODEL, D_FF),
        'ffn1_w_up': dense(ks[3], D_MODEL, D_FF),
        'ffn1_w_down': dense(ks[4], D_FF, D_MODEL),
        'ffn1_post_g': gain(ks[5], D_MODEL),
        'mix_pre_g': gain(ks[6], D_MODEL),
        'w_in': dense(ks[7], D_MODEL, IN_COLS),
        'conv_w': jax.random.normal(ks[8], (DEPTH, SSM_CONV, SSM_CONV_DIM), f32) * SSM_CONV ** -0.5,
        'conv_b': 0.02 * jax.random.normal(ks[9], (DEPTH, SSM_CONV_DIM), f32),
        'dt_bias': dt0 + jnp.log(-jnp.expm1(-dt0)),
        'a_log': jnp.log(jax.random.uniform(ks[10], (DEPTH, SSM_HEADS), f32, 1.0, 16.0)),
        'd_skip': 1.0 + 0.1 * jax.random.normal(ks[11], (DEPTH, SSM_HEADS), f32),
        'ssm_norm_g': gain(ks[12], SSM_D_INNER),
        'w_ssm_proj': dense(ks[13], SSM_D_INNER, D_MODEL),
        'attn_sinks': 0.5 * jax.random.normal(ks[14], (DEPTH, ATTN_Q_HEADS), f32),
        'rel_bias_table': 0.5 * jax.random.normal(ks[15], (REL_BUCKETS, ATTN_Q_HEADS), f32),
        'w_attn_proj': dense(ks[16], ATTN_Q_DIM, D_MODEL),
        'w_out': dense(ks[17], D_MODEL, D_MODEL),
        'mix_post_g': gain(ks[18], D_MODEL),
        'ffn2_pre_g': gain(ks[19], D_MODEL),
        'ffn2_w_gate': dense(ks[21], D_MODEL, D_FF),
        'ffn2_w_up': dense(ks[22], D_MODEL, D_FF),
        'ffn2_w_down': dense(ks[23], D_FF, D_MODEL),
        'ffn2_post_g': gain(ks[24], D_MODEL),
    }


def reference(x, ffn1_pre_g, ffn1_w_gate, ffn1_w_up, ffn1_w_down, ffn1_post_g, mix_pre_g, w_in,
              conv_w, conv_b, dt_bias, a_log, d_skip, ssm_norm_g, w_ssm_proj, attn_sinks,
              rel_bias_table, w_attn_proj, w_out, mix_post_g, ffn2_pre_g, ffn2_w_gate, ffn2_w_up,
              ffn2_w_down, ffn2_post_g):
    h = x
    for l in range(DEPTH):
        f1 = swiglu(rmsnorm(h, ffn1_pre_g[l]), ffn1_w_gate[l], ffn1_w_up[l], ffn1_w_down[l])
        h = h + FFN_RESIDUAL_WEIGHT * rmsnorm(f1, ffn1_post_g[l])
        mix = hybrid_mixer(rmsnorm(h, mix_pre_g[l]), w_in[l], conv_w[l], conv_b[l], dt_bias[l],
                           a_log[l], d_skip[l], ssm_norm_g[l], w_ssm_proj[l], attn_sinks[l],
                           rel_bias_table, w_attn_proj[l], w_out[l])
        h = h + rmsnorm(mix, mix_post_g[l])
        f2 = swiglu(rmsnorm(h, ffn2_pre_g[l]), ffn2_w_gate[l], ffn2_w_up[l], ffn2_w_down[l])
        h = h + FFN_RESIDUAL_WEIGHT * rmsnorm(f2, ffn2_post_g[l])
    return h
```

```python
import contextlib
import numpy as np
import concourse.bass as bass
import concourse.mybir as mybir
from concourse.bass_utils import run_bass_kernel_spmd

F32 = mybir.dt.float32
BF16 = mybir.dt.bfloat16
AF = mybir.ActivationFunctionType
ALU = mybir.AluOpType

D = 1024
DFF = 2816
NM = DFF // 128
T = 512
EPS = 1e-6
NCORES = 8

B_GU1 = 0
B_D1 = 11
B_XBC = 19
B_DT = 25
B_Z = 26
B_GS = 30
B_Q = 32
B_K = 34
B_V = 35
B_GA = 36
B_SP = 38
B_AP = 42
B_O = 44
B_GU2 = 46
B_D2 = 57
NBLK = 65
NSLOT = 3

SM_G = 0
SM_CW = 48
SM_CB = 144
SM_DS = 168
SM_NG = 184
NSM = 200
BC_DTB = 0
BC_ALOG = 32
BC_DSK = 64
BC_SINK = 96
BC_NG = 112
NBC = 112
C_ID = 0
C_UI = 128
C_US = 256
C_BK = 384
C_TBL = 512
C_J = 528
NCST = 656


class _Rec:
    def __init__(self):
        self.call = None

    def __getattr__(self, name):
        def f(*a, **k):
            self.call = (name, a, k)
            return None
        return f


class Sched:
    ENG = ("pe", "act", "dve", "pool", "sp")

    def __init__(self):
        self.ops = {e: [] for e in self.ENG}
        self.cnt = {e: 0 for e in self.ENG}
        self.waited = {e: {} for e in self.ENG}
        self.res = {}
        self.dma_cnt = {}
        self.regions = {}
        self.label = ""
        self.labels = {e: [] for e in self.ENG}

    def _need(self, eng, tok, waits):
        if tok is None:
            return
        key, val = tok
        if key == eng and eng == "pe":
            return
        if val > self.waited[eng].get(key, 0):
            if val > waits.get(key, 0):
                waits[key] = val

    def region(self, name):
        self.regions.setdefault(name, {})

    def alias_switch(self, name):
        bar = self.regions[name]
        dead = [k for k in self.res if isinstance(k, tuple) and k[0] == name]
        for k in dead:
            st = self.res.pop(k)
            for tok in ([st["w"]] if st["w"] else []) + st["r"]:
                if tok[1] > bar.get(tok[0], 0):
                    bar[tok[0]] = tok[1]

    def op(self, eng, fn, reads=(), writes=(), dma=None):
        waits = {}
        for k in list(reads) + list(writes):
            if isinstance(k, tuple) and k[0] in self.regions:
                for sk, v in self.regions[k[0]].items():
                    self._need(eng, (sk, v), waits)
        for r in reads:
            st = self.res.get(r)
            if st is not None:
                self._need(eng, st["w"], waits)
        for w in writes:
            st = self.res.get(w)
            if st is not None:
                self._need(eng, st["w"], waits)
                for t in st["r"]:
                    self._need(eng, t, waits)
        for k, v in waits.items():
            self.waited[eng][k] = v
        if dma is not None:
            self.dma_cnt[dma] = self.dma_cnt.get(dma, 0) + 16
            tok = (dma, self.dma_cnt[dma])
        else:
            self.cnt[eng] += 1
            tok = (eng, self.cnt[eng])
        for r in reads:
            st = self.res.setdefault(r, {"w": None, "r": []})
            st["r"].append(tok)
            if len(st["r"]) > 64:
                mx = {}
                for a, b in st["r"]:
                    if b > mx.get(a, 0):
                        mx[a] = b
                st["r"] = list(mx.items())
        for w in writes:
            self.res[w] = {"w": tok, "r": []}
        self.labels[eng].append(self.label)
        rec = _Rec()
        fn(rec)
        assert rec.call is not None
        self.ops[eng].append((rec.call, sorted(waits.items()), dma))
        return tok

    def final_wait(self, eng):
        waits = {}
        for k, v in self.dma_cnt.items():
            if v > self.waited[eng].get(k, 0):
                waits[k] = v
        for e in self.ENG:
            if e != eng and self.cnt[e] > self.waited[eng].get(e, 0):
                waits[e] = self.cnt[e]
        self.ops[eng].append((None, sorted(waits.items()), None))

    def emit(self, nc):
        with contextlib.ExitStack() as es:
            sems = {}
            for e in self.ENG:
                sems[e] = es.enter_context(nc.semaphore("s_" + e))
            for k in self.dma_cnt:
                sems[k] = es.enter_context(nc.semaphore("d_" + k))
            block = es.enter_context(nc.Block())

            def runner(ename):
                def run(eng):
                    for fn, waits, dma in self.ops[ename]:
                        for k, v in waits:
                            eng.wait_ge(sems[k], v)
                        if fn is None:
                            continue
                        name, a, k = fn
                        ins = getattr(eng, name)(*a, **k)
                        if dma is not None:
                            ins.then_inc(sems[dma], 16)
                        else:
                            ins.then_inc(sems[ename], 1)
                return run
            block.tensor(runner("pe"))
            block.scalar(runner("act"))
            block.vector(runner("dve"))
            block.gpsimd(runner("pool"))
            block.sync(runner("sp"))


def build(NSEQ, S, stage=99, dumps=()):
    NT = S // T
    nc = bass.Bass("TRN2", target_bir_lowering=False)
    xT = nc.dram_tensor("xT", [NSEQ, D, S], F32, kind="ExternalInput")
    wall = nc.dram_tensor("wall", [NBLK, 128, 4096], F32, kind="ExternalInput")
    smalls_d = nc.dram_tensor("smalls", [128, NSM], F32, kind="ExternalInput")
    bc_d = nc.dram_tensor("bcin", [128, NBC], F32, kind="ExternalInput")
    cst_d = nc.dram_tensor("cst", [128, NCST], F32, kind="ExternalInput")
    outT = nc.dram_tensor("outT", [NSEQ, D, S], F32, kind="ExternalOutput")
    wbf = nc.dram_tensor("wbf", [NBLK, 128, 4096], BF16)
    fpad = nc.dram_tensor("fpad", [16, 384], F32)

    S_ = Sched()
    op = S_.op
    for r in ("A", "F", "TT"):
        S_.region(r)

    with contextlib.ExitStack() as es:
        def sb(name, shape, dt):
            return es.enter_context(nc.sbuf_tensor(name, shape, dt))

        wslot = [sb("wslot%d" % i, [128, 4096], BF16) for i in range(NSLOT)]
        h = sb("h", [128, 8, T], F32)
        u = sb("u", [128, 8, T], BF16)
        regA = sb("regA", [128, 24 * T], BF16)
        regF = sb("regF", [128, 8 * T], F32)
        regT = sb("regT", [128, 22272], BF16)
        ynT = sb("ynT", [128, 16, T], BF16)
        kT = sb("kT", [128, 4, 128 + T], BF16)
        vaug = sb("vaug", [128, 5, 4, 65], BF16)
        Hf = sb("Hf", [128, 4, 512], F32)
        Hb = sb("Hb", [128, 4, 512], BF16)
        ebTb = sb("ebTb", [128, 16, 256], BF16)
        smalls = sb("smalls_sb", [128, NSM], F32)
        bcs = sb("bc_sb", [128, NBC], F32)
        cst = sb("cst_sb", [128, NCST], F32)
        identb = sb("identb", [128, 128], BF16)
        uinclb = sb("uinclb", [128, 128], BF16)
        ustrb = sb("ustrb", [128, 128], BF16)
        onesb = sb("onesb", [128, 128], BF16)
        onesf = sb("onesf", [128, 128], F32)
        abc = sb("abc", [128, 32], F32)
        esink = sb("esink", [128, 16], F32)
        halo = [sb("halo%d" % i, [128, 24, 3], F32) for i in range(2)]
        acc2 = sb("acc2", [128, T], F32)
        sq = [sb("sq%d" % i, [128, T], BF16) for i in range(2)]
        rt = sb("rt", [128, T], F32)
        rstd = [sb("rstd0", [128, T], F32)] * 2
        xpre = [sb("xpre%d" % i, [128, T + 3], F32) for i in range(2)]
        sg = [sb("sg%d" % i, [128, T], F32) for i in range(2)]
        acc = sg
        dtt = sb("dtt", [128, 4, 32], F32)
        dAf = sb("dAf", [128, 4, 32], F32)
        dtmp = [sb("dtmp%d" % i, [128, 4, 32], F32) for i in range(3)]
        dtw = [sb("dtw%d" % i, [128, 32], F32) for i in range(2)]
        small1 = sb("small1", [128, 16], F32)
        ebt16 = sb("ebt16", [16, 128], F32)
        Ddiag = sb("Ddiag", [128, 16, 128], BF16)
        zero16 = sb("zero16", [16, 128], F32)

        banks = [es.enter_context(nc.psum_tensor("bank%d" % i, [128, 512], F32)) for i in range(8)]
        bank_ctr = [0]

        def nbank():
            i = bank_ctr[0] % 8
            bank_ctr[0] += 1
            return i

        actv = regA[:, 0:NM * T].rearrange("p (m t) -> p m t", t=T)
        xbc = regA[:, 0:24 * T].rearrange("p (m t) -> p m t", t=T)
        qT = regA[:, 0:8 * T].rearrange("p (m t) -> p m t", t=T)
        oT = regA[:, 8 * T:16 * T].rearrange("p (m t) -> p m t", t=T)
        mgb = regA[:, 16 * T:24 * T].rearrange("p (m t) -> p m t", t=T)
        fbuf = regF[:, :].rearrange("p (m t) -> p m t", t=T)
        szv = regF[:, :].bitcast(BF16).rearrange("p (b c) -> p b c", c=2048)

        def tview(off, n, dt=BF16):
            v = regT[:, off:off + n]
            return v.bitcast(F32) if dt == F32 else v
        x_tok = [tview(0, 2048), tview(2048, 2048)]
        B_tok = [tview(4096, 512), tview(4608, 512)]
        dAU = [tview(5120 + i * 1024, 1024) for i in range(2)]
        dec = [tview(7168 + i * 1024, 1024) for i in range(2)]
        LT = [tview(9216 + i * 1024, 1024) for i in range(2)]
        smg = [tview(11264 + i * 128, 128) for i in range(2)]
        yA = [tview(11520 + i * 1024, 1024, F32) for i in range(2)]
        yB = [tview(13568 + i * 1024, 1024, F32) for i in range(3)]
        ynb = [tview(16640 + i * 512, 512) for i in range(2)]
        junkb = tview(17664, 512)
        xw_all = [tview(18176, 2048), tview(20224, 2048)]
        Pf = [tview(i * 1024, 1024, F32) for i in range(4)]
        PTb = [tview(4096 + i * 512, 512) for i in range(4)]
        otok = [tview(6144 + i * 1024, 1024) for i in range(2)]
        ebTf = tview(0, 8192, F32).rearrange("p (h c) -> p h c", c=256)

        def smc(col):
            return smalls[:, col:col + 1]

        cgroups = [(19, 21), (21, 23), (23, 26), (26, 30), (30, 34), (34, B_SP), (B_SP + 4, B_GU2)]
        def cast_group(gi_, m):
            b0, b1 = cgroups[gi_]
            src = wall[b0:b1].rearrange("b p (x c) -> (b p x) c", c=2048)
            dst = wbf[b0:b1].rearrange("b p (x c) -> (b p x) c", c=2048)
            op("pool", lambda e: e.dma_start(out=dst, in_=src), reads=[("A", "act", m)],
               writes=[("wbf", b) for b in range(b0, b1)], dma="cast%d" % gi_)
        cast_at = {10: 0, 12: 1, 14: 2, 16: 3, 18: 4, 20: 5, 21: 6}

        op("sp", lambda e: e.dma_start(out=smalls[:, :], in_=smalls_d.ap()), writes=["smalls"], dma="c0")
        op("sp", lambda e: e.dma_start(out=bcs[:, :], in_=bc_d.ap()), writes=["bcs"], dma="c1")
        op("sp", lambda e: e.dma_start(out=cst[:, :], in_=cst_d.ap()), writes=["cst"], dma="c2")
        op("dve", lambda e: e.tensor_copy(out=identb[:, :], in_=cst[:, C_ID:C_ID + 128]), reads=["cst"], writes=["identb"])
        op("dve", lambda e: e.tensor_copy(out=uinclb[:, :], in_=cst[:, C_UI:C_UI + 128]), reads=["cst"], writes=["uinclb"])
        op("dve", lambda e: e.tensor_copy(out=ustrb[:, :], in_=cst[:, C_US:C_US + 128]), reads=["cst"], writes=["ustrb"])
        op("pool", lambda e: e.memset(onesb[:, :], 1.0), writes=["onesb"])
        op("pool", lambda e: e.memset(onesf[:, :], 1.0), writes=["onesf"])
        op("pool", lambda e: e.memset(zero16[:, :], 0.0), writes=["zero16"])
        op("pool", lambda e: e.memset(vaug[:, :, :, :], 1.0), writes=[("va", i) for i in range(5)])
        op("act", lambda e: e.activation(out=abc[:, :], in_=bcs[:, BC_ALOG:BC_ALOG + 32], func=AF.Exp),
           reads=["bcs"], writes=["abc"])
        op("dve", lambda e: e.tensor_scalar(out=abc[:, :], in0=abc[:, :], scalar1=-1.0, scalar2=None, op0=ALU.mult),
           reads=["abc"], writes=["abc"])
        op("act", lambda e: e.activation(out=esink[:, :], in_=bcs[:, BC_SINK:BC_SINK + 16], func=AF.Exp),
           reads=["bcs"], writes=["esink"])
        for c in range(16):
            op("act", lambda e, c=c: e.activation(out=Ddiag[:, c, :], in_=identb[:, :], func=AF.Identity, scale=smc(SM_DS + c)),
               reads=["identb", "smalls"], writes=["Ddiag"])
        dump_ctr = [0]

        def dump(name, ap, reads):
            if name not in dumps:
                return
            dt_ = ap.dtype
            d = nc.dram_tensor("dbg_" + name, list(ap.shape), dt_, kind="ExternalOutput")
            dump_ctr[0] += 1
            op("sp", lambda e, d=d, ap=ap: e.dma_start(out=d.ap(), in_=ap), reads=reads, dma="dbg%d" % dump_ctr[0])

        wq = []
        wstate = {"n": 0}

        stg = [regT[:, 0:4096].bitcast(F32), regT[:, 4096:8192].bitcast(F32)]

        def wload(blk, ncols=4096):
            i = wstate["n"]
            wstate["n"] += 1
            s = i % NSLOT
            if wstate.get("direct"):
                for hf, c0 in enumerate(range(0, ncols, 2048)):
                    c1 = min(ncols, c0 + 2048)
                    k = wstate["stg"] = (wstate.get("stg", -1) + 1) % 2
                    op("sp", lambda e: e.dma_start(out=stg[k][:, 0:c1 - c0], in_=wall[blk, :, c0:c1]), writes=[("TT", "stg", k)], dma="wd%d" % k)
                    op("act", lambda e: e.activation(out=wslot[s][:, c0:c1], in_=stg[k][:, 0:c1 - c0], func=AF.Copy),
                       reads=[("TT", "stg", k)], writes=[("ws", s)])
                op("act", lambda e: e.dma_start(out=wbf[blk, :, 0:ncols], in_=wslot[s][:, 0:ncols]), reads=[("ws", s)], writes=[("wbf", blk)], dma="wst%d" % s)
                return s
            op("sp", lambda e, s=s, blk=blk, ncols=ncols: e.dma_start(out=wslot[s][:, 0:ncols], in_=wbf[blk, :, 0:ncols]),
               reads=[("wbf", blk)], writes=[("ws", s)], dma="w%d" % s)
            return s

        def norm_stats(chunks, scale, bias=EPS):
            pb = nbank()
            n = len(chunks)
            for c, (ap, rk) in enumerate(chunks):
                sqt = sq[c % 2]
                op("act", lambda e, ap=ap, sqt=sqt: e.activation(out=sqt[:, :], in_=ap, func=AF.Square),
                   reads=[rk], writes=[("sq", c % 2)])
                op("pe", lambda e, sqt=sqt, c=c, pb=pb: e.matmul(banks[pb][:, :], lhsT=onesb[:, :], rhs=sqt[:, :],
                                                                start=(c == 0), stop=(c == n - 1)),
                   reads=[("sq", c % 2), "onesb"], writes=[("ps", pb)])
            op("act", lambda e, pb=pb: e.activation(out=rt[:, :], in_=banks[pb][:, :], func=AF.Ln, bias=bias, scale=scale),
               reads=[("ps", pb)], writes=["rt"])
            ri = 0
            op("act", lambda e, ri=ri: e.activation(out=rstd[ri][:, :], in_=rt[:, :], func=AF.Exp, scale=-0.5), reads=["rt"], writes=[("rstd", ri)])
            return rstd[ri], ("rstd", ri)
        norm_stats.ctr = 0

        def prenorm(gcol, src=None):
            if src is None:
                src = [(h[:, c, :], ("h", c)) for c in range(8)]
            r, rk = norm_stats(src, 1.0 / D)
            for c in range(8):
                ap, k_ = src[c]
                op("dve", lambda e, c=c, r=r, ap=ap: e.scalar_tensor_tensor(out=u[:, c, :], in0=ap, scalar=smc(gcol + c),
                                                                             in1=r[:, :], op0=ALU.mult, op1=ALU.mult),
                   reads=[k_, rk, "smalls"], writes=[("u", c)])

        def postnorm_residual(gcol, weight, eps_mult=1.0, after=None):
            r, rk = norm_stats([(fbuf[:, c, :], ("F", "f", c)) for c in range(8)], 1.0 / (D * weight * weight), eps_mult * EPS / (weight * weight))
            for c in range(8):
                tmp = sg[c % 2]
                op("dve", lambda e, c=c, r=r, tmp=tmp: e.scalar_tensor_tensor(out=tmp[:, :], in0=fbuf[:, c, :], scalar=smc(gcol + c),
                                                                               in1=r[:, :], op0=ALU.mult, op1=ALU.mult),
                   reads=[("F", "f", c), rk, "smalls"], writes=[("sg", c % 2)])
                op("pool", lambda e, c=c, tmp=tmp: e.tensor_tensor(out=h[:, c, :], in0=h[:, c, :], in1=tmp[:, :], op=ALU.add),
                   reads=[("sg", c % 2), ("h", c)], writes=[("h", c)])
                if after is not None:
                    after(c)

        def fm_proj(pb, src, nk, wap_fn, wreads, src_reads):
            for k in range(nk):
                op("pe", lambda e, k=k: e.matmul(banks[pb][:, :], lhsT=wap_fn(k), rhs=src[:, k, :], start=(k == 0), stop=(k == nk - 1)),
                   reads=list(wreads) + [src_reads(k)], writes=[("ps", pb)])

        def ffn_begin(which, gpre, b_gu, b_d, src=None):
            S_.label = "FFN%d.%d" % (which, tile_id[0])
            ctx = {"blks": [(b_gu + j, 4096) for j in range(11)] + [(b_d + dc, 2816) for dc in range(8)], "slots": {}, "which": which}

            def prefetch(i):
                if i < len(ctx["blks"]) and i not in ctx["slots"]:
                    ctx["slots"][i] = wload(*ctx["blks"][i])
            ctx["prefetch"] = prefetch
            prefetch(0)
            prefetch(1)
            prenorm(gpre, src)
            return ctx

        def ffn_mid(ctx):
            S_.label = "FFN%d.%d" % (ctx["which"], tile_id[0])
            prefetch, slots = ctx["prefetch"], ctx["slots"]
            S_.alias_switch("A")
            for j in range(11):
                prefetch(j + 2)
                s = slots[j]
                wv = wslot[s][:, :].rearrange("p (mm gu k c) -> p mm gu k c", mm=2, gu=2, k=8)
                for mm in range(2):
                    m = 2 * j + mm
                    pg = nbank()
                    fm_proj(pg, u, 8, lambda k, wv=wv, mm=mm: wv[:, mm, 0, k, :], [("ws", s)], lambda k: ("u", k))
                    pu = nbank()
                    fm_proj(pu, u, 8, lambda k, wv=wv, mm=mm: wv[:, mm, 1, k, :], [("ws", s)], lambda k: ("u", k))
                    sgt = sg[m % 2]
                    op("act", lambda e, pg=pg, sgt=sgt: e.activation(out=sgt[:, :], in_=banks[pg][:, :], func=AF.Silu),
                       reads=[("ps", pg)], writes=[("sg", m % 2)])
                    op("dve", lambda e, pu=pu, sgt=sgt, m=m: e.tensor_tensor(out=actv[:, m, :], in0=banks[pu][:, :], in1=sgt[:, :], op=ALU.mult),
                       reads=[("ps", pu), ("sg", m % 2)], writes=[("A", "act", m)])
                    if "hook" in ctx:
                        ctx["hook"](m)
            S_.alias_switch("F")
            for dc in range(8):
                prefetch(11 + dc + 2)
                s = slots[11 + dc]
                wv = wslot[s][:, 0:2816].rearrange("p (m c) -> p m c", c=128)
                pb = nbank()
                fm_proj(pb, actv, NM, lambda k, wv=wv: wv[:, k, :], [("ws", s)], lambda k: ("A", "act", k))
                op("dve", lambda e, pb=pb, dc=dc: e.tensor_copy(out=fbuf[:, dc, :], in_=banks[pb][:, :]),
                   reads=[("ps", pb)], writes=[("F", "f", dc)])

        def ffn_post(ctx, gpost, after=None):
            S_.label = "FFN%d.%d" % (ctx["which"], tile_id[0])
            postnorm_residual(gpost, 0.5, after=after)


        def tm_proj(pb, c0, ncol, tb, wap_fn, wreads):
            for k in range(8):
                op("pe", lambda e, k=k: e.matmul(banks[pb][:, c0:c0 + ncol], lhsT=u[:, k, tb * 128:(tb + 1) * 128], rhs=wap_fn(k),
                                                 start=(k == 0), stop=(k == 7)),
                   reads=list(wreads) + [("u", k)], writes=[("ps", pb)])

        tile_id = [0]

        def mixer(first):
            S_.label = "M1.%d" % tile_id[0]
            prenorm(SM_G + 16)
            S_.alias_switch("A")
            S_.alias_switch("F")
            S_.alias_switch("TT")
            if first:
                hz = halo[tile_id[0] % 2]
                op("pool", lambda e: e.memset(hz[:, :, :], 0.0), writes=[("halo%d" % (tile_id[0] % 2), c) for c in range(24)])
                op("pool", lambda e: e.memset(Hf[:, :, :], 0.0), writes=[("Hf", g) for g in range(4)])
                op("pool", lambda e: e.memset(Hb[:, :, :], 0.0), writes=[("Hb", g) for g in range(4)])
                op("pool", lambda e: e.memset(kT[:, :, 0:128], 0.0), writes=[("kT", kv) for kv in range(4)])
                op("pool", lambda e: e.memset(vaug[:, 0, :, 0:64], 0.0), writes=[("va", 0)])
            S_.label = "M2.%d" % tile_id[0]
            hin = halo[tile_id[0] % 2]
            hout = halo[(tile_id[0] + 1) % 2]
            hik = "halo%d" % (tile_id[0] % 2)
            hok = "halo%d" % ((tile_id[0] + 1) % 2)
            acc3 = [sg[0], sg[1], acc2]
            acck = [("sg", 0), ("sg", 1), "acc2"]
            m2 = {}

            def m2_A(c):
                j, cc = divmod(c, 4)
                if cc == 0:
                    m2["s"] = wload(B_XBC + j)
                sl = m2["s"]
                wv = wslot[sl][:, :].rearrange("p (k c) -> p k c", c=512)
                pb = nbank()
                fm_proj(pb, u, 8, lambda k: wv[:, k, cc * 128:(cc + 1) * 128], [("ws", sl)], lambda k: ("u", k))
                P = banks[pb]
                a_ = acc3[c % 3]
                xp = xpre[c % 2]
                xk = ("xpre", c % 2)
                op("act", lambda e: e.activation(out=xp[:, 3:T + 3], in_=P[:, :], func=AF.Copy), reads=[("ps", pb)], writes=[xk])
                op("act", lambda e: e.activation(out=a_[:, :], in_=P[:, :], func=AF.Identity, bias=smc(SM_CB + c), scale=smc(SM_CW + 3 * 24 + c)),
                   reads=[("ps", pb), "smalls"], writes=[acck[c % 3]])
                op("pool", lambda e: e.tensor_copy(out=xp[:, 0:3], in_=hin[:, c, :]), reads=[(hik, c), xk], writes=[xk])
                op("pool", lambda e: e.tensor_copy(out=hout[:, c, :], in_=xp[:, T:T + 3]), reads=[xk], writes=[(hok, c)])

            def m2_B(c):
                a_ = acc3[c % 3]
                ak = acck[c % 3]
                xp = xpre[c % 2]
                xk = ("xpre", c % 2)
                for sh in (1, 2, 3):
                    op("dve", lambda e: e.scalar_tensor_tensor(out=a_[:, :], in0=xp[:, 3 - sh:T + 3 - sh], scalar=smc(SM_CW + (3 - sh) * 24 + c),
                                                               in1=a_[:, :], op0=ALU.mult, op1=ALU.add),
                       reads=[xk, ak, "smalls"], writes=[ak])

            def m2_C(c):
                a_ = acc3[c % 3]
                op("act", lambda e: e.activation(out=xbc[:, c, :], in_=a_[:, :], func=AF.Silu),
                   reads=[acck[c % 3]], writes=[("A", "xbc", c)])

            for c in range(24 + 2):
                if 0 <= c - 2 < 24:
                    m2_C(c - 2)
                if c < 24:
                    m2_A(c)
                if 0 <= c - 1 < 24:
                    m2_B(c - 1)
            dump("xbc_%d" % tile_id[0], xbc, [("A", "xbc", c) for c in range(24)])
            S_.label = "M3.%d" % tile_id[0]
            s = wload(B_DT)
            wv = wslot[s][:, :].rearrange("p (k c) -> p k c", c=512)
            pb = nbank()
            for tb in range(4):
                tm_proj(pb, tb * 32, 32, tb, lambda k, wv=wv: wv[:, k, 0:32], [("ws", s)])
            dps = banks[pb][:, 0:128].rearrange("p (b h) -> p b h", h=32)
            dtb_bc = bcs[:, BC_DTB:BC_DTB + 32].unsqueeze(1).broadcast_to([128, 4, 32])
            a_bc4 = abc[:, :].unsqueeze(1).broadcast_to([128, 4, 32])
            op("dve", lambda e: e.tensor_tensor(out=dtmp[0][:, :, :], in0=dps, in1=dtb_bc, op=ALU.add),
               reads=[("ps", pb), "bcs"], writes=["dtmp0"])
            op("dve", lambda e: e.scalar_tensor_tensor(out=dtmp[1][:, :, :], in0=dtmp[0][:, :, :], scalar=-1.0, in1=dtmp[0][:, :, :],
                                                        op0=ALU.mult, op1=ALU.min),
               reads=["dtmp0"], writes=["dtmp1"])
            op("act", lambda e: e.activation(out=dtmp[1][:, :, :], in_=dtmp[1][:, :, :], func=AF.Exp),
               reads=["dtmp1"], writes=["dtmp1"])
            op("act", lambda e: e.activation(out=dtmp[2][:, :, :], in_=dtmp[1][:, :, :], func=AF.Ln, bias=1.0),
               reads=["dtmp1"], writes=["dtmp2"])
            op("dve", lambda e: e.scalar_tensor_tensor(out=dtt[:, :, :], in0=dtmp[0][:, :, :], scalar=0.0, in1=dtmp[2][:, :, :],
                                                        op0=ALU.max, op1=ALU.add),
               reads=["dtmp0", "dtmp2"], writes=["dtt"])
            op("dve", lambda e: e.tensor_tensor(out=dAf[:, :, :], in0=dtt[:, :, :], in1=a_bc4, op=ALU.mult),
               reads=["dtt", "abc"], writes=["dAf"])
            dump("dtt_%d" % tile_id[0], dtt[:, :, :], ["dtt"])
            S_.label = "M4.%d" % tile_id[0]
            for j in range(4):
                s = wload(B_Z + j)
                wv = wslot[s][:, :].rearrange("p (k c) -> p k c", c=512)
                for tb in range(4):
                    pb = nbank()
                    tm_proj(pb, 0, 512, tb, lambda k, wv=wv: wv[:, k, :], [("ws", s)])
                    op("act", lambda e, pb=pb, tb=tb, j=j: e.activation(out=szv[:, tb, j * 512:(j + 1) * 512], in_=banks[pb][:, :], func=AF.Silu),
                       reads=[("ps", pb)], writes=[("F", "sz", tb, j)])
            dump("sz_%d" % tile_id[0], szv, [("F", "sz", tb, j) for tb in range(4) for j in range(4)])
            S_.label = "M5.%d" % tile_id[0]
            D_bc = bcs[:, BC_DSK:BC_DSK + 32]

            def tsl(tb):
                return slice(tb * 128, (tb + 1) * 128)

            def chunk_pre(tb):
                cp = tb % 2
                ts_ = tsl(tb)
                xs_banks = []
                for half in range(2):
                    pb = nbank()
                    xs_banks.append(pb)
                    pbv = banks[pb][:, :].bitcast(BF16)
                    for cc in range(8):
                        c = half * 8 + cc
                        op("pe", lambda e: e.transpose(pbv[:, cc * 128:(cc + 1) * 128], xbc[:, c, ts_], identb[:, :]),
                           reads=[("A", "xbc", c), "identb"], writes=[("ps", pb)])
                    hs = slice(half * 16, half * 16 + 16)
                    pv3 = pbv.rearrange("p (h d) -> p h d", d=64)
                    op("dve", lambda e: e.tensor_tensor(
                        out=x_tok[cp][:, half * 1024:(half + 1) * 1024].rearrange("p (h d) -> p h d", d=64), in0=pv3,
                        in1=dtt[:, tb, hs].unsqueeze(2).broadcast_to([128, 16, 64]), op=ALU.mult),
                       reads=[("ps", pb), "dtt"], writes=[("TT", "x_tok", cp, half)])
                pb = nbank()
                pbv = banks[pb][:, :].bitcast(BF16)
                for g in range(4):
                    op("pe", lambda e: e.transpose(pbv[:, g * 128:(g + 1) * 128], xbc[:, 16 + g, ts_], identb[:, :]),
                       reads=[("A", "xbc", 16 + g), "identb"], writes=[("ps", pb)])
                op("act", lambda e: e.activation(out=B_tok[cp][:, :], in_=pbv[:, 0:512], func=AF.Copy),
                   reads=[("ps", pb)], writes=[("TT", "B_tok", cp)])
                pc = nbank()
                for i, (lt, lk) in enumerate(((cst[:, C_UI:C_UI + 128], "cst"), (cst[:, C_US:C_US + 128], "cst"), (onesf[:, :], "onesf"))):
                    op("pe", lambda e: e.matmul(banks[pc][:, i * 32:(i + 1) * 32], lhsT=lt, rhs=dAf[:, tb, :], start=True, stop=True),
                       reads=[lk, "dAf"], writes=[("ps", pc)])
                exps = dtmp[cp][:, 0:3, :]
                op("act", lambda e: e.activation(out=exps, in_=banks[pc][:, 0:96].rearrange("p (a h) -> p a h", h=32), func=AF.Exp),
                   reads=[("ps", pc)], writes=["dtmp%d" % cp])
                op("dve", lambda e: e.tensor_tensor(out=dtw[cp][:, :], in0=dtt[:, tb, :], in1=dtmp[cp][:, 1, :], op=ALU.mult),
                   reads=["dtt", "dtmp%d" % cp], writes=["dtw%d" % cp])
                for half in range(2):
                    pb = xs_banks[half]
                    pv3 = banks[pb][:, :].bitcast(BF16).rearrange("p (h d) -> p h d", d=64)
                    hs = slice(half * 16, half * 16 + 16)
                    op("dve", lambda e: e.tensor_tensor(
                        out=xw_all[cp][:, half * 1024:(half + 1) * 1024].rearrange("p (h d) -> p h d", d=64), in0=pv3,
                        in1=dtw[cp][:, hs].unsqueeze(2).broadcast_to([128, 16, 64]), op=ALU.mult),
                       reads=[("ps", pb), "dtw%d" % cp], writes=[("TT", "xw", cp, half)])

            def gi(n):
                tb, g = divmod(n, 4)
                return tb, g, tsl(tb), tb % 2, slice(8 * g, 8 * g + 8)

            def dve_dAU(n):
                tb, g, ts_, cp, h8 = gi(n)
                i2 = n % 2
                op("pool" if n % 2 == 0 else "dve", lambda e: e.tensor_tensor(
                    out=dAU[i2][:, :].rearrange("p (h i) -> p h i", i=128),
                    in0=dAf[:, tb, h8].unsqueeze(2).broadcast_to([128, 8, 128]),
                    in1=uinclb[:, :].unsqueeze(1).broadcast_to([128, 8, 128]), op=ALU.mult),
                   reads=["dAf", "uinclb"], writes=[("TT", "dAU", i2)])

            def dve_xw(n):
                tb, g, ts_, cp, h8 = gi(n)
                i2 = n % 2
                op("dve", lambda e: e.tensor_tensor(
                    out=xw[i2][:, :].rearrange("p (h d) -> p h d", d=64),
                    in0=x_tok[cp][:, g * 512:(g + 1) * 512].rearrange("p (h d) -> p h d", d=64),
                    in1=dtmp[cp][:, 1, h8].unsqueeze(2).broadcast_to([128, 8, 64]), op=ALU.mult),
                   reads=[("TT", "x_tok", cp, g // 2), "dtmp%d" % cp], writes=[("TT", "xw", i2)])

            def act_stats(n):
                tb, g, ts_, cp, h8 = gi(n)
                i3 = n % 3
                op("act", lambda e: e.activation(out=junkb, in_=yB[i3][:, :], func=AF.Square, scale=512.0 ** -0.5, accum_out=small1[:, g:g + 1]),
                   reads=[("TT", "yB", i3)], writes=[("TT", "junk"), ("small1", g)])
                op("act", lambda e: e.activation(out=small1[:, 4 + g:5 + g], in_=small1[:, g:g + 1], func=AF.Ln, bias=EPS),
                   reads=[("small1", g)], writes=[("small1", 4 + g)])
                op("act", lambda e: e.activation(out=small1[:, 8 + g:9 + g], in_=small1[:, 4 + g:5 + g], func=AF.Exp, scale=-0.5),
                   reads=[("small1", 4 + g)], writes=[("small1", 8 + g)])

            def dve_stt(n):
                tb, g, ts_, cp, h8 = gi(n)
                i3 = n % 3
                i2 = n % 2
                op("act", lambda e: e.activation(out=ynb[i2][:, :], in_=yB[i3][:, :], func=AF.Copy, scale=small1[:, 8 + g:9 + g]),
                   reads=[("TT", "yB", i3), ("small1", 8 + g)], writes=[("TT", "ynb", i2)])

            def pool_gate(n):
                tb, g, ts_, cp, h8 = gi(n)
                i3 = n % 3
                op("dve", lambda e: e.tensor_tensor(out=yB[i3][:, :], in0=yB[i3][:, :], in1=szv[:, tb, g * 512:(g + 1) * 512], op=ALU.mult),
                   reads=[("TT", "yB", i3), ("F", "sz", tb, g)], writes=[("TT", "yB", i3)])

            def act_Hb(n):
                tb, g, ts_, cp, h8 = gi(n)
                op("act", lambda e: e.activation(out=Hb[:, g, :], in_=Hf[:, g, :], func=AF.Copy),
                   reads=[("Hf", g)], writes=[("Hb", g)])

            seg_banks = {}

            def pe_seg(n):
                tb, g, ts_, cp, h8 = gi(n)
                i2 = n % 2
                seg_banks[n] = []
                for hh in range(2):
                    pg = nbank()
                    seg_banks[n].append(pg)
                    op("pe", lambda e: e.matmul(banks[pg][:, :], lhsT=ustrb[:, :], rhs=dAU[i2][:, hh * 512:(hh + 1) * 512], start=True, stop=True),
                       reads=["ustrb", ("TT", "dAU", i2)], writes=[("ps", pg)])

            def act_dec(n):
                i2 = n % 2
                for hh in range(2):
                    pg = seg_banks[n][hh]
                    op("act", lambda e: e.activation(out=dec[i2][:, hh * 512:(hh + 1) * 512], in_=banks[pg][:, :], func=AF.Exp),
                       reads=[("ps", pg)], writes=[("TT", "dec", i2, hh)])

            p2_banks = {}

            def pe_p2(n):
                tb, g, ts_, cp, h8 = gi(n)
                i2 = n % 2
                pyd = nbank()
                for cc in range(4):
                    c = 4 * g + cc
                    op("pe", lambda e: e.matmul(banks[pyd][:, cc * 128:(cc + 1) * 128], lhsT=xbc[:, c, ts_], rhs=Ddiag[:, c, :], start=True, stop=False),
                       reads=[("A", "xbc", c), "Ddiag"], writes=[("ps", pyd)])
                    for hh in range(2):
                        hl = 2 * cc + hh
                        hg = 8 * g + hl
                        op("pe", lambda e: e.matmul(banks[pyd][:, hl * 64:(hl + 1) * 64], lhsT=LT[i2][:, hl * 128:(hl + 1) * 128],
                                                    rhs=x_tok[cp][:, hg * 64:(hg + 1) * 64], start=False, stop=(hh == 1)),
                           reads=[("TT", "LT", i2), ("TT", "x_tok", cp, hg // 16)], writes=[("ps", pyd)])
                pyo = nbank()
                op("pe", lambda e: e.matmul(banks[pyo][:, :], lhsT=xbc[:, 20 + g, ts_], rhs=Hb[:, g, :], start=True, stop=True),
                   reads=[("A", "xbc", 20 + g), ("Hb", g)], writes=[("ps", pyo)])
                pst = nbank()
                op("pe", lambda e: e.matmul(banks[pst][:, :], lhsT=B_tok[cp][:, g * 128:(g + 1) * 128], rhs=xw_all[cp][:, g * 512:(g + 1) * 512], start=True, stop=True),
                   reads=[("TT", "B_tok", cp), ("TT", "xw", cp, g // 2)], writes=[("ps", pst)])
                p2_banks[n] = (pyd, pyo, pst)

            def pool_Hscale(n):
                tb, g, ts_, cp, h8 = gi(n)
                op("pool", lambda e: e.tensor_tensor(
                    out=Hf[:, g, :].rearrange("p (h d) -> p h d", d=64), in0=Hf[:, g, :].rearrange("p (h d) -> p h d", d=64),
                    in1=dtmp[cp][:, 2, h8].unsqueeze(2).broadcast_to([128, 8, 64]), op=ALU.mult),
                   reads=[("Hf", g), "dtmp%d" % cp], writes=[("Hf", g)])

            def pool_LT(n):
                i2 = n % 2
                op("pool", lambda e: e.tensor_tensor(
                    out=LT[i2][:, :].rearrange("p (h i) -> p h i", i=128),
                    in0=dec[i2][:, :].rearrange("p (h i) -> p h i", i=128),
                    in1=smg[i2][:, :].unsqueeze(1).broadcast_to([128, 8, 128]), op=ALU.mult),
                   reads=[("TT", "dec", i2, 0), ("TT", "dec", i2, 1), ("TT", "smg", i2)], writes=[("TT", "LT", i2)])

            tr_banks = {}

            def pe_tr(n):
                i2 = n % 2
                ptr = nbank()
                tr_banks[n] = ptr
                ptv = banks[ptr][:, :].bitcast(BF16)
                for cc in range(4):
                    op("pe", lambda e: e.transpose(ptv[:, cc * 128:(cc + 1) * 128], ynb[i2][:, cc * 128:(cc + 1) * 128], identb[:, :]),
                       reads=[("TT", "ynb", i2), "identb"], writes=[("ps", ptr)])

            def act_copy(n):
                tb, g, ts_, cp, h8 = gi(n)
                ptr = tr_banks[n]
                ptv = banks[ptr][:, :].bitcast(BF16)
                op("act", lambda e: e.activation(out=ynT[:, 4 * g:4 * g + 4, ts_], in_=ptv[:, 0:512].rearrange("p (c t) -> p c t", t=128), func=AF.Copy),
                   reads=[("ps", ptr)], writes=[("ynT", 4 * g + cc) for cc in range(4)])

            sc_banks = {}

            def pe_scores(n):
                tb, g, ts_, cp, h8 = gi(n)
                psc = nbank()
                sc_banks[n] = psc
                op("pe", lambda e: e.matmul(banks[psc][:, 0:128], lhsT=xbc[:, 16 + g, ts_], rhs=xbc[:, 20 + g, ts_], start=True, stop=True),
                   reads=[("A", "xbc", 16 + g), ("A", "xbc", 20 + g)], writes=[("ps", psc)])

            def dve_smg(n):
                i2 = n % 2
                psc = sc_banks[n]
                op("dve", lambda e: e.tensor_tensor(out=smg[i2][:, :], in0=banks[psc][:, 0:128], in1=cst[:, C_UI:C_UI + 128], op=ALU.mult),
                   reads=[("ps", psc), "cst"], writes=[("TT", "smg", i2)])

            def dve_combine(n):
                tb, g, ts_, cp, h8 = gi(n)
                i2 = n % 2
                i3 = n % 3
                pyd, pyo, pst = p2_banks[n]
                op("dve", lambda e: e.tensor_tensor(
                    out=yA[i2][:, :].rearrange("p (h d) -> p h d", d=64),
                    in0=banks[pyo][:, :].rearrange("p (h d) -> p h d", d=64),
                    in1=dtmp[cp][:, 0, h8].unsqueeze(2).broadcast_to([128, 8, 64]), op=ALU.mult),
                   reads=[("ps", pyo), "dtmp%d" % cp], writes=[("TT", "yA", i2)])
                op("dve", lambda e: e.tensor_tensor(out=yB[i3][:, :], in0=banks[pyd][:, :], in1=yA[i2][:, :], op=ALU.add),
                   reads=[("ps", pyd), ("TT", "yA", i2)], writes=[("TT", "yB", i3)])
                op("dve", lambda e: e.tensor_tensor(out=Hf[:, g, :], in0=banks[pst][:, :], in1=Hf[:, g, :], op=ALU.add),
                   reads=[("ps", pst), ("Hf", g)], writes=[("Hf", g)])

            kvp = {}

            def kv_piece(i):
                if i == 0:
                    kvp["sk"] = wload(B_K)
                if i == 4:
                    kvp["sv"] = wload(B_V)
                if i < 4:
                    kv = i
                    s_ = kvp["sk"]
                    wv = wslot[s_][:, :].rearrange("p (k c) -> p k c", c=512)
                    pb = nbank()
                    fm_proj(pb, u, 8, lambda k: wv[:, k, kv * 128:(kv + 1) * 128], [("ws", s_)], lambda k: ("u", k))
                    op("act", lambda e: e.activation(out=kT[:, kv, 128:128 + T], in_=banks[pb][:, :], func=AF.Copy),
                       reads=[("ps", pb)], writes=[("kT", kv)])
                else:
                    tb = i - 4
                    s_ = kvp["sv"]
                    wv = wslot[s_][:, :].rearrange("p (k c) -> p k c", c=512)
                    pb = nbank()
                    tm_proj(pb, 0, 256, tb, lambda k: wv[:, k, 0:256], [("ws", s_)])
                    op("act", lambda e: e.activation(out=vaug[:, 1 + tb, :, 0:64], in_=banks[pb][:, 0:256].rearrange("p (a d) -> p a d", d=64), func=AF.Copy),
                       reads=[("ps", pb)], writes=[("va", 1 + tb)])

            NG = 16

            def ok(m):
                return 0 <= m < NG

            chunk_pre(0)
            for it in range(NG + 5):
                if it >= 2 and (it - 2) % 4 == 0 and (it - 2) // 4 + 1 < 4:
                    chunk_pre((it - 2) // 4 + 1)
                if ok(it): dve_dAU(it)
                if ok(it - 4): act_stats(it - 4); dve_stt(it - 4)
                if ok(it - 3): pool_gate(it - 3); act_Hb(it - 3)
                if ok(it - 1): pe_seg(it - 1); act_dec(it - 1)
                if ok(it - 2): pe_p2(it - 2); pool_Hscale(it - 2)
                if ok(it - 1): pool_LT(it - 1)
                if ok(it - 5): pe_tr(it - 5)
                if ok(it): pe_scores(it); dve_smg(it)
                if ok(it - 2): dve_combine(it - 2)
                if ok(it - 5): act_copy(it - 5)
                if 4 <= it < 12:
                    kv_piece(it - 4)
            dump("ynT_%d" % tile_id[0], ynT[:, :, :], [("ynT", c) for c in range(16)])
            S_.label = "M6.%d" % tile_id[0]
            S_.alias_switch("F")
            m6 = {"sgs": None, "ssp": None}

            def m6_half(dc, half):
                if half == 0:
                    if dc % 2 == 0:
                        m6["ssp"] = wload(B_SP + dc // 2)
                    if dc % 4 == 0:
                        m6["sgs"] = wload(B_GS + dc // 4)
                    m6["py"] = nbank()
                ssp, sgs, py = m6["ssp"], m6["sgs"], m6["py"]
                wsp_v = wslot[ssp][:, :].rearrange("p (d k c) -> p d k c", d=2, k=16)
                wgs_v = wslot[sgs][:, :].rearrange("p (k c) -> p k c", c=512)
                ks = range(0, 12) if half == 0 else range(12, 16)
                for k in ks:
                    op("pe", lambda e: e.matmul(banks[py][:, :], lhsT=wsp_v[:, dc % 2, k, :], rhs=ynT[:, k, :], start=(k == 0), stop=(k == 15)),
                       reads=[("ws", ssp), ("ynT", k)], writes=[("ps", py)])
                if half == 1:
                    pg = nbank()
                    fm_proj(pg, u, 8, lambda k: wgs_v[:, k, (dc % 4) * 128:(dc % 4 + 1) * 128], [("ws", sgs)], lambda k: ("u", k))
                    sgt = sg[dc % 2]
                    op("act", lambda e: e.activation(out=sgt[:, :], in_=banks[pg][:, :], func=AF.Tanh, scale=0.5),
                       reads=[("ps", pg)], writes=[("sg", dc % 2)])
                    op("dve", lambda e: e.scalar_tensor_tensor(out=fbuf[:, dc, :], in0=sgt[:, :], scalar=1.0, in1=banks[py][:, :],
                                                               op0=ALU.add, op1=ALU.mult),
                       reads=[("ps", py), ("sg", dc % 2)], writes=[("F", "f", dc)])
            S_.label = "M7.%d" % tile_id[0]
            S_.alias_switch("A")
            S_.alias_switch("TT")
            for j in range(2):
                s = wload(B_Q + j)
                wv = wslot[s][:, :].rearrange("p (k c) -> p k c", c=512)
                for cc in range(4):
                    c = 4 * j + cc
                    pb = nbank()
                    fm_proj(pb, u, 8, lambda k, wv=wv, cc=cc: wv[:, k, cc * 128:(cc + 1) * 128], [("ws", s)], lambda k: ("u", k))
                    op("act", lambda e, pb=pb, c=c: e.activation(out=qT[:, c, :], in_=banks[pb][:, :], func=AF.Copy, scale=0.125),
                       reads=[("ps", pb)], writes=[("A", "qT", c)])
            at = {}

            def pe_scores_a(n):
                qb, kv = divmod(n, 4)
                qs = slice(qb * 128, (qb + 1) * 128)
                kcur = slice(128 + qb * 128, 128 + (qb + 1) * 128)
                kprev = slice(qb * 128, (qb + 1) * 128)
                at[n] = []
                for par in range(2):
                    pb = nbank()
                    at[n].append(pb)
                    base = par * 64
                    for jj in range(2):
                        hq = 4 * kv + 2 * jj + par
                        qc = hq // 2
                        op("pe", lambda e: e.matmul(banks[pb][:, jj * 256:jj * 256 + 128], lhsT=kT[base:base + 64, kv, kcur],
                                                    rhs=qT[base:base + 64, qc, qs], start=True, stop=True),
                           reads=[("kT", kv), ("A", "qT", qc)], writes=[("ps", pb)])
                        op("pe", lambda e: e.matmul(banks[pb][:, jj * 256 + 128:jj * 256 + 256], lhsT=kT[base:base + 64, kv, kprev],
                                                    rhs=qT[base:base + 64, qc, qs], start=True, stop=True),
                           reads=[("kT", kv), ("A", "qT", qc)], writes=[("ps", pb)])

            def act_exp_a(n):
                ip = n % 2
                for par in range(2):
                    pb = at[n][par]
                    pi = ip * 2 + par
                    op("act", lambda e: e.activation(out=Pf[pi][:, :], in_=banks[pb][:, :], func=AF.Exp),
                       reads=[("ps", pb)], writes=[("TT", "Pf", pi)])

            def pt_mult(n):
                qb, kv = divmod(n, 4)
                ip = n % 2
                for par, eng in ((1, "pool"), (0, "dve")):
                    pi = ip * 2 + par
                    h0 = 4 * kv + par
                    op(eng, lambda e: e.tensor_tensor(out=PTb[pi][:, :].rearrange("p (a c) -> p a c", c=256),
                                                      in0=Pf[pi][:, :].rearrange("p (a c) -> p a c", c=256),
                                                      in1=ebTb[:, h0:h0 + 3:2, :], op=ALU.mult),
                       reads=[("TT", "Pf", pi), "ebTb"], writes=[("TT", "PTb", pi)])

            pvb = {}

            def pe_pv(n):
                qb, kv = divmod(n, 4)
                ip = n % 2
                skip_prev = first and qb == 0
                po = nbank()
                pvb[n] = po
                for jl in range(4):
                    par = jl % 2
                    jj = jl // 2
                    pi = ip * 2 + par
                    ov = banks[po][:, jl * 65:(jl + 1) * 65]
                    if not skip_prev:
                        op("pe", lambda e: e.matmul(ov, lhsT=PTb[pi][:, jj * 256 + 128:jj * 256 + 256], rhs=vaug[:, qb, kv, :], start=True, stop=False),
                           reads=[("TT", "PTb", pi), ("va", qb)], writes=[("ps", po)])
                    op("pe", lambda e: e.matmul(ov, lhsT=PTb[pi][:, jj * 256:jj * 256 + 128], rhs=vaug[:, qb + 1, kv, :], start=skip_prev, stop=True),
                       reads=[("TT", "PTb", pi), ("va", qb + 1)], writes=[("ps", po)])

            def dve_norm(n):
                qb, kv = divmod(n, 4)
                ot = otok[qb % 2]
                po = pvb[n]
                o3 = banks[po][:, 0:260].rearrange("p (a d) -> p a d", d=65)
                den = small1[:, 12:16]
                op("dve", lambda e: e.tensor_tensor(out=den, in0=o3[:, :, 64], in1=esink[:, 4 * kv:4 * kv + 4], op=ALU.add),
                   reads=[("ps", po), "esink"], writes=[("small1", "den")])
                op("dve", lambda e: e.reciprocal(out=den, in_=den), reads=[("small1", "den")], writes=[("small1", "den")])
                op("dve", lambda e: e.tensor_tensor(
                    out=ot[:, kv * 256:(kv + 1) * 256].rearrange("p (a d) -> p a d", d=64), in0=o3[:, :, 0:64],
                    in1=den.unsqueeze(2).broadcast_to([128, 4, 64]), op=ALU.mult),
                   reads=[("ps", po), ("small1", "den")], writes=[("TT", "otok", qb % 2, kv)])

            def A3(qb):
                qs = slice(qb * 128, (qb + 1) * 128)
                ot = otok[qb % 2]
                ptr = nbank()
                ptv = banks[ptr][:, :].bitcast(BF16)
                for c in range(8):
                    op("pe", lambda e: e.transpose(ptv[:, c * 128:(c + 1) * 128], ot[:, c * 128:(c + 1) * 128], identb[:, :]),
                       reads=[("TT", "otok", qb % 2, c // 2), "identb"], writes=[("ps", ptr)])
                op("act", lambda e: e.activation(out=oT[:, :, qs], in_=ptv.rearrange("p (c t) -> p c t", t=128), func=AF.Copy),
                   reads=[("ps", ptr)], writes=[("A", "oT", c) for c in range(8)])

            for it in range(16 + 2):
                if it < 16:
                    pe_scores_a(it)
                    act_exp_a(it)
                if 0 <= it - 1 < 16:
                    pe_pv(it - 1)
                    dve_norm(it - 1)
                if it < 16:
                    S_.label = "M6.%d" % tile_id[0]
                    m6_half(it // 2, it % 2)
                    S_.label = "M7.%d" % tile_id[0]
                    pt_mult(it)
                if it - 2 >= 0 and (it - 2) % 4 == 3:
                    A3((it - 2) // 4)
            dump("oT_%d" % tile_id[0], oT, [("A", "oT", c) for c in range(8)])
            op("pool", lambda e: e.tensor_copy(out=kT[:, :, 0:128], in_=kT[:, :, T:T + 128]),
               reads=[("kT", kv) for kv in range(4)], writes=[("kT", kv) for kv in range(4)])
            op("pool", lambda e: e.tensor_copy(out=vaug[:, 0, :, 0:64], in_=vaug[:, 4, :, 0:64]),
               reads=[("va", 4)], writes=[("va", 0)])
            S_.label = "M8.%d" % tile_id[0]
            sga = None
            sap = None
            for dc in range(8):
                if dc % 4 == 0:
                    sap = wload(B_AP + dc // 4)
                    sga = wload(B_GA + dc // 4)
                wap_v = wslot[sap][:, :].rearrange("p (d k c) -> p d k c", d=4, k=8)
                wga_v = wslot[sga][:, :].rearrange("p (k c) -> p k c", c=512)
                pg = nbank()
                fm_proj(pg, u, 8, lambda k, wga_v=wga_v, dc=dc: wga_v[:, k, (dc % 4) * 128:(dc % 4 + 1) * 128], [("ws", sga)], lambda k: ("u", k))
                py = nbank()
                fm_proj(py, oT, 8, lambda k, wap_v=wap_v, dc=dc: wap_v[:, dc % 4, k, :], [("ws", sap)], lambda k: ("A", "oT", k))
                sgt = sg[dc % 2]
                op("act", lambda e, pg=pg, sgt=sgt: e.activation(out=sgt[:, :], in_=banks[pg][:, :], func=AF.Tanh, scale=0.5),
                   reads=[("ps", pg)], writes=[("sg", dc % 2)])
                op("dve", lambda e, py=py, sgt=sgt: e.scalar_tensor_tensor(out=sgt[:, :], in0=sgt[:, :], scalar=1.0, in1=banks[py][:, :],
                                                                          op0=ALU.add, op1=ALU.mult),
                   reads=[("ps", py), ("sg", dc % 2)], writes=[("sg", dc % 2)])
                op("pool", lambda e, sgt=sgt, dc=dc: e.tensor_tensor(out=mgb[:, dc, :], in0=fbuf[:, dc, :], in1=sgt[:, :], op=ALU.add),
                   reads=[("F", "f", dc), ("sg", dc % 2)], writes=[("A", "mgb", dc)])
            dump("mgb_%d" % tile_id[0], mgb, [("A", "mgb", c) for c in range(8)])
            S_.label = "M9.%d" % tile_id[0]
            for dc in range(8):
                if dc % 4 == 0:
                    so = wload(B_O + dc // 4)
                wo_v = wslot[so][:, :].rearrange("p (d k c) -> p d k c", d=4, k=8)
                pb = nbank()
                fm_proj(pb, mgb, 8, lambda k, wo_v=wo_v, dc=dc: wo_v[:, dc % 4, k, :], [("ws", so)], lambda k: ("A", "mgb", k))
                op("dve", lambda e, pb=pb, dc=dc: e.tensor_copy(out=fbuf[:, dc, :], in_=banks[pb][:, :]),
                   reads=[("ps", pb)], writes=[("F", "f", dc)])
            postnorm_residual(SM_G + 24, 1.0, eps_mult=4.0)

        def setup_late():
            S_.alias_switch("TT")
            wtmp = regT[:, 0:8192].bitcast(F32)
            wout = regT[:, 16384:20480]
            for j in range(4):
                op("sp", lambda e, j=j: e.dma_start(out=wtmp, in_=wall[B_SP + j]), writes=[("TT", "wtmp")], dma="c7")
                wt4 = wtmp.rearrange("p (d k c) -> p d k c", d=2, k=16)
                wo4 = wout.rearrange("p (d k c) -> p d k c", d=2, k=16)
                for kc in range(16):
                    op("dve", lambda e, kc=kc, wt4=wt4, wo4=wo4: e.tensor_scalar(out=wo4[:, :, kc, :], in0=wt4[:, :, kc, :], scalar1=smc(SM_NG + kc),
                                                                              scalar2=None, op0=ALU.mult),
                       reads=[("TT", "wtmp"), "smalls"], writes=[("TT", "wout")])
                op("sp", lambda e, j=j: e.dma_start(out=wbf[B_SP + j], in_=wout), reads=[("TT", "wout")], writes=[("wbf", B_SP + j)], dma="c8")
            S_.alias_switch("TT")
            pb = nbank()
            op("pe", lambda e, pb=pb: e.matmul(banks[pb][0:16, 0:128], lhsT=cst[0:32, C_TBL:C_TBL + 16],
                                               rhs=cst[0:32, C_BK:C_BK + 128], start=True, stop=True),
               reads=["cst"], writes=[("ps", pb)])
            op("act", lambda e, pb=pb: e.activation(out=ebt16[:, :], in_=banks[pb][0:16, 0:128], func=AF.Exp),
               reads=[("ps", pb)], writes=["ebt16"])
            op("sp", lambda e: e.dma_start(out=fpad[:, 0:128], in_=zero16[:, :]), reads=["zero16"], writes=["fpad0"], dma="c3")
            op("sp", lambda e: e.dma_start(out=fpad[:, 256:384], in_=zero16[:, :]), reads=["zero16"], writes=["fpad1"], dma="c4")
            op("sp", lambda e: e.dma_start(out=fpad[:, 128:256], in_=ebt16[:, :]), reads=["ebt16"], writes=["fpad2"], dma="c5")
            skew = bass.AP(fpad, 1, [[1, 128], [384, 16], [1, 256]])
            op("sp", lambda e: e.dma_start(out=ebTf, in_=skew), reads=["fpad0", "fpad1", "fpad2"],
               writes=[("TT", "ebTf")], dma="c6")
            for hp in range(8):
                pb = nbank()
                op("pe", lambda e, pb=pb, hp=hp: e.matmul(banks[pb][:, :], lhsT=cst[:, C_J:C_J + 128],
                                                          rhs=ebTf[:, 2 * hp:2 * hp + 2, :], start=True, stop=True),
                   reads=["cst", ("TT", "ebTf")], writes=[("ps", pb)])
                op("dve", lambda e, pb=pb, hp=hp: e.tensor_copy(out=ebTb[:, 2 * hp:2 * hp + 2, :],
                                                                in_=banks[pb][:, :].rearrange("p (h c) -> p h c", c=256)),
                   reads=[("ps", pb)], writes=["ebTb"])
            S_.alias_switch("TT")


        def h_store(si, t0, c):
            op("act", lambda e: e.dma_start(out=outT[si, c * 128:(c + 1) * 128, t0:t0 + T], in_=h[:, c, :]), reads=[("h", c)], dma="hs%d" % c)

        def h_load(si, t0, c):
            op("act", lambda e: e.dma_start(out=h[:, c, :], in_=xT[si, c * 128:(c + 1) * 128, t0:t0 + T]), writes=[("h", c)], dma="hl%d" % c)

        xstg = regT[:, 0:16384].bitcast(F32).rearrange("p (c t) -> p c t", t=T)

        def x_stage(si, t0, c):
            op("act", lambda e: e.dma_start(out=xstg[:, c, :], in_=xT[si, c * 128:(c + 1) * 128, t0:t0 + T]), writes=[("TT", "xs", c)], dma="hl%d" % c)

        tiles = [(si, it) for si in range(NSEQ) for it in range(NT)]
        ctx1 = None
        for idx, (si, it) in enumerate(tiles):
            t0 = it * T
            first = (it == 0)
            nxt = tiles[idx + 1] if idx + 1 < len(tiles) else None
            tile_id[0] = idx
            if idx == 0:
                for c in range(8):
                    h_load(si, t0, c)
            wstate["direct"] = (idx == 0)
            if ctx1 is None:
                ctx1 = ffn_begin(1, SM_G + 0, B_GU1, B_D1)
            if idx == 0:
                ctx1["hook"] = lambda m: cast_group(cast_at[m], m) if m in cast_at else None
            ffn_mid(ctx1)
            wstate["direct"] = False
            ffn_post(ctx1, SM_G + 8)
            ctx1 = None
            if idx == 0:
                S_.label = "SETUP"
                setup_late()
            if stage >= 2:
                mixer(first)
            if idx == 0:
                S_.alias_switch("TT")
            wstate["direct"] = (idx == 0)
            ctx2 = ffn_begin(2, SM_G + 32, B_GU2, B_D2)
            hoist = nxt is not None and idx != 0
            if hoist:
                S_.alias_switch("TT")
                for c in range(8):
                    x_stage(nxt[0], nxt[1] * T, c)
            ffn_mid(ctx2)
            wstate["direct"] = False
            if hoist:
                tile_id[0] = idx + 1
                ctx1 = ffn_begin(1, SM_G + 0, B_GU1, B_D1, src=[(xstg[:, c, :], ("TT", "xs", c)) for c in range(8)])
                tile_id[0] = idx

            def after(c, si=si, t0=t0, nxt=nxt, hoist=hoist):
                h_store(si, t0, c)
                if nxt is not None and not hoist:
                    h_load(nxt[0], nxt[1] * T, c)
            ffn_post(ctx2, SM_G + 40, after=after)
            if hoist:
                for c in range(8):
                    op("pool", lambda e, c=c: e.tensor_copy(out=h[:, c, :], in_=xstg[:, c, :]), reads=[("TT", "xs", c)], writes=[("h", c)])
        S_.final_wait("sp")
        S_.emit(nc)
    nc._sched_labels = S_.labels
    return nc


def _t5_bucket(d):
    import math
    max_exact = 16
    dd = np.maximum(d, 1).astype(np.float32)
    large = max_exact + (np.log(dd / max_exact) / math.log(128 / max_exact) * (32 - max_exact)).astype(np.int32)
    large = np.minimum(large, 31)
    return np.where(d < max_exact, d, large)


def _blk_kc(w, ncols_pad=512):
    n = w.shape[1]
    out = np.zeros((128, 8, ncols_pad), np.float32)
    out[:, :, :n] = w.reshape(8, 128, n).transpose(1, 0, 2)
    return out.reshape(128, 8 * ncols_pad)


def prep_shared(inp):
    f = np.float32
    wall = np.zeros((NBLK, 128, 4096), f)

    def put_gu(base, wg, wu):
        wg = wg.reshape(8, 128, NM, 128)
        wu = wu.reshape(8, 128, NM, 128)
        for j in range(11):
            blk = np.zeros((128, 2, 2, 8, 128), f)
            for mm in range(2):
                m = 2 * j + mm
                blk[:, mm, 0] = wg[:, :, m, :].transpose(1, 0, 2)
                blk[:, mm, 1] = wu[:, :, m, :].transpose(1, 0, 2)
            wall[base + j] = blk.reshape(128, 4096)

    def put_d(base, wd):
        wd = wd.reshape(NM, 128, 8, 128)
        for dc in range(8):
            wall[base + dc, :, :2816] = wd[:, :, dc, :].transpose(1, 0, 2).reshape(128, 2816)

    put_gu(B_GU1, inp["ffn1_w_gate"][0], inp["ffn1_w_up"][0])
    put_d(B_D1, inp["ffn1_w_down"][0])
    put_gu(B_GU2, inp["ffn2_w_gate"][0], inp["ffn2_w_up"][0])
    put_d(B_D2, inp["ffn2_w_down"][0])
    win = inp["w_in"][0]
    o = 0
    gs = win[:, o:o + 1024]; o += 1024
    ga = win[:, o:o + 1024]; o += 1024
    z = win[:, o:o + 2048]; o += 2048
    xbc = win[:, o:o + 3072]; o += 3072
    dt = win[:, o:o + 32]; o += 32
    q = win[:, o:o + 1024]; o += 1024
    k = win[:, o:o + 256]; o += 256
    v = win[:, o:o + 256]; o += 256
    for j in range(6):
        wall[B_XBC + j] = _blk_kc(xbc[:, j * 512:(j + 1) * 512])
    wall[B_DT] = _blk_kc(dt)
    for j in range(4):
        wall[B_Z + j] = _blk_kc(z[:, j * 512:(j + 1) * 512])
    for j in range(2):
        wall[B_GS + j] = _blk_kc(gs[:, j * 512:(j + 1) * 512])
        wall[B_GA + j] = _blk_kc(ga[:, j * 512:(j + 1) * 512])
        wall[B_Q + j] = _blk_kc(q[:, j * 512:(j + 1) * 512])
    kd = np.concatenate([np.concatenate([k[:, i * 64:(i + 1) * 64]] * 2, axis=1) for i in range(4)], axis=1)
    wall[B_K] = _blk_kc(kd)
    wall[B_V] = _blk_kc(v)
    wsp = inp["w_ssm_proj"][0].reshape(16, 128, 8, 128)
    for j in range(4):
        blk = np.zeros((128, 2, 16, 128), f)
        for d2 in range(2):
            blk[:, d2] = wsp[:, :, 2 * j + d2, :].transpose(1, 0, 2)
        wall[B_SP + j] = blk.reshape(128, 4096)
    for base, name in ((B_AP, "w_attn_proj"), (B_O, "w_out")):
        w = inp[name][0].reshape(8, 128, 8, 128)
        for j in range(2):
            blk = np.zeros((128, 4, 8, 128), f)
            for d4 in range(4):
                blk[:, d4] = w[:, :, 4 * j + d4, :].transpose(1, 0, 2)
            wall[base + j] = blk.reshape(128, 4096)

    smalls = np.zeros((128, NSM), f)
    for i, name in enumerate(("ffn1_pre_g", "ffn1_post_g", "mix_pre_g", "mix_post_g", "ffn2_pre_g", "ffn2_post_g")):
        smalls[:, SM_G + 8 * i:SM_G + 8 * i + 8] = inp[name][0].reshape(8, 128).T
    cw = inp["conv_w"][0]
    for kk in range(4):
        smalls[:, SM_CW + kk * 24:SM_CW + kk * 24 + 24] = cw[kk].reshape(24, 128).T
    smalls[:, SM_CB:SM_CB + 24] = inp["conv_b"][0].reshape(24, 128).T
    smalls[:, SM_DS:SM_DS + 16] = np.repeat(inp["d_skip"][0], 64).reshape(16, 128).T
    smalls[:, SM_NG:SM_NG + 16] = inp["ssm_norm_g"][0].reshape(16, 128).T

    bc = np.zeros((128, NBC), f)
    bc[:, BC_DTB:BC_DTB + 32] = inp["dt_bias"][0][None, :]
    bc[:, BC_ALOG:BC_ALOG + 32] = inp["a_log"][0][None, :]
    bc[:, BC_DSK:BC_DSK + 32] = inp["d_skip"][0][None, :]
    bc[:, BC_SINK:BC_SINK + 16] = inp["attn_sinks"][0][None, :]

    cst = np.zeros((128, NCST), f)
    idx = np.arange(128)
    cst[:, C_ID:C_ID + 128] = np.eye(128, dtype=f)
    cst[:, C_UI:C_UI + 128] = (idx[:, None] <= idx[None, :]).astype(f)
    cst[:, C_US:C_US + 128] = (idx[:, None] > idx[None, :]).astype(f)
    bk = _t5_bucket(idx)
    cst[0:32, C_BK:C_BK + 128] = (np.arange(32)[:, None] == bk[None, :]).astype(f)
    cst[0:32, C_TBL:C_TBL + 16] = inp["rel_bias_table"]
    cst[:, C_J:C_J + 128] = (idx[:, None] + idx[None, :] == 127).astype(f)
    return {"wall": wall, "smalls": smalls, "bcin": bc, "cst": cst}


_NC_CACHE = {}


def kernel(**inputs):
    inp = {k: np.asarray(v) for k, v in inputs.items()}
    x = inp["x"]
    B, S, _ = x.shape
    nseq = B // NCORES
    shared = prep_shared(inp)
    key = (nseq, S)
    if key not in _NC_CACHE:
        _NC_CACHE[key] = build(nseq, S)
    nc = _NC_CACHE[key]
    in_maps = []
    for c in range(NCORES):
        xT = np.ascontiguousarray(x[c * nseq:(c + 1) * nseq].transpose(0, 2, 1))
        m = dict(shared)
        m["xT"] = xT
        in_maps.append(m)
    res = run_bass_kernel_spmd(nc, in_maps, core_ids=list(range(NCORES)))
    out = np.empty((B, S, D), np.float32)
    for c in range(NCORES):
        out[c * nseq:(c + 1) * nseq] = res.results[c]["outT"].transpose(0, 2, 1)
    return out
```

```python
import contextlib
import numpy as np
import concourse.bass as bass
import concourse.mybir as mybir
from concourse.bass_utils import run_bass_kernel_spmd

F32 = mybir.dt.float32
BF16 = mybir.dt.bfloat16
AF = mybir.ActivationFunctionType
ALU = mybir.AluOpType

D = 1024
DFF = 2816
NM = DFF // 128
T = 512
EPS = 1e-6
NCORES = 8

B_GU1 = 0
B_D1 = 11
B_XBC = 19
B_DT = 25
B_Z = 26
B_GS = 30
B_Q = 32
B_K = 34
B_V = 35
B_GA = 36
B_SP = 38
B_AP = 42
B_O = 44
B_GU2 = 46
B_D2 = 57
NBLK = 65
NSLOT = 3

SM_G = 0
SM_CW = 48
SM_CB = 144
SM_DS = 168
SM_NG = 184
NSM = 200
BC_DTB = 0
BC_ALOG = 32
BC_DSK = 64
BC_SINK = 96
BC_NG = 112
NBC = 112
C_ID = 0
C_UI = 128
C_US = 256
C_BK = 384
C_TBL = 512
C_J = 528
NCST = 656


class _Rec:
    def __init__(self):
        self.call = None

    def __getattr__(self, name):
        def f(*a, **k):
            self.call = (name, a, k)
            return None
        return f


class Sched:
    ENG = ("pe", "act", "dve", "pool", "sp")

    def __init__(self):
        self.ops = {e: [] for e in self.ENG}
        self.cnt = {e: 0 for e in self.ENG}
        self.waited = {e: {} for e in self.ENG}
        self.res = {}
        self.dma_cnt = {}
        self.regions = {}
        self.label = ""
        self.labels = {e: [] for e in self.ENG}

    def _need(self, eng, tok, waits):
        if tok is None:
            return
        key, val = tok
        if key == eng and eng == "pe":
            return
        if val > self.waited[eng].get(key, 0):
            if val > waits.get(key, 0):
                waits[key] = val

    def region(self, name):
        self.regions.setdefault(name, {})

    def alias_switch(self, name):
        bar = self.regions[name]
        dead = [k for k in self.res if isinstance(k, tuple) and k[0] == name]
        for k in dead:
            st = self.res.pop(k)
            for tok in ([st["w"]] if st["w"] else []) + st["r"]:
                if tok[1] > bar.get(tok[0], 0):
                    bar[tok[0]] = tok[1]

    def op(self, eng, fn, reads=(), writes=(), dma=None):
        waits = {}
        for k in list(reads) + list(writes):
            if isinstance(k, tuple) and k[0] in self.regions:
                for sk, v in self.regions[k[0]].items():
                    self._need(eng, (sk, v), waits)
        for r in reads:
            st = self.res.get(r)
            if st is not None:
                self._need(eng, st["w"], waits)
        for w in writes:
            st = self.res.get(w)
            if st is not None:
                self._need(eng, st["w"], waits)
                for t in st["r"]:
                    self._need(eng, t, waits)
        for k, v in waits.items():
            self.waited[eng][k] = v
        if dma is not None:
            self.dma_cnt[dma] = self.dma_cnt.get(dma, 0) + 16
            tok = (dma, self.dma_cnt[dma])
        else:
            self.cnt[eng] += 1
            tok = (eng, self.cnt[eng])
        for r in reads:
            st = self.res.setdefault(r, {"w": None, "r": []})
            st["r"].append(tok)
            if len(st["r"]) > 64:
                mx = {}
                for a, b in st["r"]:
                    if b > mx.get(a, 0):
                        mx[a] = b
                st["r"] = list(mx.items())
        for w in writes:
            self.res[w] = {"w": tok, "r": []}
        self.labels[eng].append(self.label)
        rec = _Rec()
        fn(rec)
        assert rec.call is not None
        self.ops[eng].append((rec.call, sorted(waits.items()), dma))
        return tok

    def final_wait(self, eng):
        waits = {}
        for k, v in self.dma_cnt.items():
            if v > self.waited[eng].get(k, 0):
                waits[k] = v
        for e in self.ENG:
            if e != eng and self.cnt[e] > self.waited[eng].get(e, 0):
                waits[e] = self.cnt[e]
        self.ops[eng].append((None, sorted(waits.items()), None))

    def emit(self, nc):
        with contextlib.ExitStack() as es:
            sems = {}
            for e in self.ENG:
                sems[e] = es.enter_context(nc.semaphore("s_" + e))
            for k in self.dma_cnt:
                sems[k] = es.enter_context(nc.semaphore("d_" + k))
            block = es.enter_context(nc.Block())

            def runner(ename):
                def run(eng):
                    for fn, waits, dma in self.ops[ename]:
                        for k, v in waits:
                            eng.wait_ge(sems[k], v)
                        if fn is None:
                            continue
                        name, a, k = fn
                        ins = getattr(eng, name)(*a, **k)
                        if dma is not None:
                            ins.then_inc(sems[dma], 16)
                        else:
                            ins.then_inc(sems[ename], 1)
                return run
            block.tensor(runner("pe"))
            block.scalar(runner("act"))
            block.vector(runner("dve"))
            block.gpsimd(runner("pool"))
            block.sync(runner("sp"))


def build(NSEQ, S, stage=99, dumps=()):
    NT = S // T
    nc = bass.Bass("TRN2", target_bir_lowering=False)
    xT = nc.dram_tensor("xT", [NSEQ, D, S], F32, kind="ExternalInput")
    wall = nc.dram_tensor("wall", [NBLK, 128, 4096], F32, kind="ExternalInput")
    smalls_d = nc.dram_tensor("smalls", [128, NSM], F32, kind="ExternalInput")
    bc_d = nc.dram_tensor("bcin", [128, NBC], F32, kind="ExternalInput")
    cst_d = nc.dram_tensor("cst", [128, NCST], F32, kind="ExternalInput")
    outT = nc.dram_tensor("outT", [NSEQ, D, S], F32, kind="ExternalOutput")
    wbf = nc.dram_tensor("wbf", [NBLK, 128, 4096], BF16)
    fpad = nc.dram_tensor("fpad", [16, 384], F32)

    S_ = Sched()
    op = S_.op
    for r in ("A", "F", "TT"):
        S_.region(r)

    with contextlib.ExitStack() as es:
        def sb(name, shape, dt):
            return es.enter_context(nc.sbuf_tensor(name, shape, dt))

        wslot = [sb("wslot%d" % i, [128, 4096], BF16) for i in range(NSLOT)]
        h = sb("h", [128, 8, T], F32)
        u = sb("u", [128, 8, T], BF16)
        regA = sb("regA", [128, 24 * T], BF16)
        regF = sb("regF", [128, 8 * T], F32)
        regT = sb("regT", [128, 22272], BF16)
        ynT = sb("ynT", [128, 16, T], BF16)
        kT = sb("kT", [128, 4, 128 + T], BF16)
        vaug = sb("vaug", [128, 5, 4, 65], BF16)
        Hf = sb("Hf", [128, 4, 512], F32)
        Hb = sb("Hb", [128, 4, 512], BF16)
        ebTb = sb("ebTb", [128, 16, 256], BF16)
        smalls = sb("smalls_sb", [128, NSM], F32)
        bcs = sb("bc_sb", [128, NBC], F32)
        cst = sb("cst_sb", [128, NCST], F32)
        identb = sb("identb", [128, 128], BF16)
        uinclb = sb("uinclb", [128, 128], BF16)
        ustrb = sb("ustrb", [128, 128], BF16)
        onesb = sb("onesb", [128, 128], BF16)
        onesf = sb("onesf", [128, 128], F32)
        abc = sb("abc", [128, 32], F32)
        esink = sb("esink", [128, 16], F32)
        halo = [sb("halo%d" % i, [128, 24, 3], F32) for i in range(2)]
        acc2 = sb("acc2", [128, T], F32)
        sq = [sb("sq%d" % i, [128, T], BF16) for i in range(2)]
        rt = sb("rt", [128, T], F32)
        rstd = [sb("rstd0", [128, T], F32)] * 2
        xpre = [sb("xpre%d" % i, [128, T + 3], F32) for i in range(2)]
        sg = [sb("sg%d" % i, [128, T], F32) for i in range(2)]
        acc = sg
        dtt = sb("dtt", [128, 4, 32], F32)
        dAf = sb("dAf", [128, 4, 32], F32)
        dtmp = [sb("dtmp%d" % i, [128, 4, 32], F32) for i in range(3)]
        dtw = [sb("dtw%d" % i, [128, 32], F32) for i in range(2)]
        small1 = sb("small1", [128, 16], F32)
        ebt16 = sb("ebt16", [16, 128], F32)
        Ddiag = sb("Ddiag", [128, 16, 128], BF16)
        zero16 = sb("zero16", [16, 128], F32)

        banks = [es.enter_context(nc.psum_tensor("bank%d" % i, [128, 512], F32)) for i in range(8)]
        bank_ctr = [0]

        def nbank():
            i = bank_ctr[0] % 8
            bank_ctr[0] += 1
            return i

        actv = regA[:, 0:NM * T].rearrange("p (m t) -> p m t", t=T)
        xbc = regA[:, 0:24 * T].rearrange("p (m t) -> p m t", t=T)
        qT = regA[:, 0:8 * T].rearrange("p (m t) -> p m t", t=T)
        oT = regA[:, 8 * T:16 * T].rearrange("p (m t) -> p m t", t=T)
        mgb = regA[:, 16 * T:24 * T].rearrange("p (m t) -> p m t", t=T)
        fbuf = regF[:, :].rearrange("p (m t) -> p m t", t=T)
        szv = regF[:, :].bitcast(BF16).rearrange("p (b c) -> p b c", c=2048)

        def tview(off, n, dt=BF16):
            v = regT[:, off:off + n]
            return v.bitcast(F32) if dt == F32 else v
        x_tok = [tview(0, 2048), tview(2048, 2048)]
        B_tok = [tview(4096, 512), tview(4608, 512)]
        dAU = [tview(5120 + i * 1024, 1024) for i in range(2)]
        dec = [tview(7168 + i * 1024, 1024) for i in range(2)]
        LT = [tview(9216 + i * 1024, 1024) for i in range(2)]
        smg = [tview(11264 + i * 128, 128) for i in range(2)]
        yA = [tview(11520 + i * 1024, 1024, F32) for i in range(2)]
        yB = [tview(13568 + i * 1024, 1024, F32) for i in range(3)]
        ynb = [tview(16640 + i * 512, 512) for i in range(2)]
        junkb = tview(17664, 512)
        xw_all = [tview(18176, 2048), tview(20224, 2048)]
        Pf = [tview(i * 1024, 1024, F32) for i in range(4)]
        PTb = [tview(4096 + i * 512, 512) for i in range(4)]
        otok = [tview(6144 + i * 1024, 1024) for i in range(2)]
        ebTf = tview(0, 8192, F32).rearrange("p (h c) -> p h c", c=256)

        def smc(col):
            return smalls[:, col:col + 1]

        cgroups = [(19, 21), (21, 23), (23, 26), (26, 30), (30, 34), (34, B_SP), (B_SP + 4, B_GU2)]
        def cast_group(gi_, m):
            b0, b1 = cgroups[gi_]
            src = wall[b0:b1].rearrange("b p (x c) -> (b p x) c", c=2048)
            dst = wbf[b0:b1].rearrange("b p (x c) -> (b p x) c", c=2048)
            op("pool", lambda e: e.dma_start(out=dst, in_=src), reads=[("A", "act", m)],
               writes=[("wbf", b) for b in range(b0, b1)], dma="cast%d" % gi_)
        cast_at = {10: 0, 12: 1, 14: 2, 16: 3, 18: 4, 20: 5, 21: 6}

        op("sp", lambda e: e.dma_start(out=smalls[:, :], in_=smalls_d.ap()), writes=["smalls"], dma="c0")
        op("sp", lambda e: e.dma_start(out=bcs[:, :], in_=bc_d.ap()), writes=["bcs"], dma="c1")
        op("sp", lambda e: e.dma_start(out=cst[:, :], in_=cst_d.ap()), writes=["cst"], dma="c2")
        op("dve", lambda e: e.tensor_copy(out=identb[:, :], in_=cst[:, C_ID:C_ID + 128]), reads=["cst"], writes=["identb"])
        op("dve", lambda e: e.tensor_copy(out=uinclb[:, :], in_=cst[:, C_UI:C_UI + 128]), reads=["cst"], writes=["uinclb"])
        op("dve", lambda e: e.tensor_copy(out=ustrb[:, :], in_=cst[:, C_US:C_US + 128]), reads=["cst"], writes=["ustrb"])
        op("pool", lambda e: e.memset(onesb[:, :], 1.0), writes=["onesb"])
        op("pool", lambda e: e.memset(onesf[:, :], 1.0), writes=["onesf"])
        op("pool", lambda e: e.memset(zero16[:, :], 0.0), writes=["zero16"])
        op("pool", lambda e: e.memset(vaug[:, :, :, :], 1.0), writes=[("va", i) for i in range(5)])
        op("act", lambda e: e.activation(out=abc[:, :], in_=bcs[:, BC_ALOG:BC_ALOG + 32], func=AF.Exp),
           reads=["bcs"], writes=["abc"])
        op("dve", lambda e: e.tensor_scalar(out=abc[:, :], in0=abc[:, :], scalar1=-1.0, scalar2=None, op0=ALU.mult),
           reads=["abc"], writes=["abc"])
        op("act", lambda e: e.activation(out=esink[:, :], in_=bcs[:, BC_SINK:BC_SINK + 16], func=AF.Exp),
           reads=["bcs"], writes=["esink"])
        for c in range(16):
            op("act", lambda e, c=c: e.activation(out=Ddiag[:, c, :], in_=identb[:, :], func=AF.Identity, scale=smc(SM_DS + c)),
               reads=["identb", "smalls"], writes=["Ddiag"])
        dump_ctr = [0]

        def dump(name, ap, reads):
            if name not in dumps:
                return
            dt_ = ap.dtype
            d = nc.dram_tensor("dbg_" + name, list(ap.shape), dt_, kind="ExternalOutput")
            dump_ctr[0] += 1
            op("sp", lambda e, d=d, ap=ap: e.dma_start(out=d.ap(), in_=ap), reads=reads, dma="dbg%d" % dump_ctr[0])

        wq = []
        wstate = {"n": 0}

        stg = [regT[:, 0:4096].bitcast(F32), regT[:, 4096:8192].bitcast(F32)]

        def wload(blk, ncols=4096):
            i = wstate["n"]
            wstate["n"] += 1
            s = i % NSLOT
            if wstate.get("direct"):
                for hf, c0 in enumerate(range(0, ncols, 2048)):
                    c1 = min(ncols, c0 + 2048)
                    k = wstate["stg"] = (wstate.get("stg", -1) + 1) % 2
                    op("sp", lambda e: e.dma_start(out=stg[k][:, 0:c1 - c0], in_=wall[blk, :, c0:c1]), writes=[("TT", "stg", k)], dma="wd%d" % k)
                    op("act", lambda e: e.activation(out=wslot[s][:, c0:c1], in_=stg[k][:, 0:c1 - c0], func=AF.Copy),
                       reads=[("TT", "stg", k)], writes=[("ws", s)])
                op("act", lambda e: e.dma_start(out=wbf[blk, :, 0:ncols], in_=wslot[s][:, 0:ncols]), reads=[("ws", s)], writes=[("wbf", blk)], dma="wst%d" % s)
                return s
            op("sp", lambda e, s=s, blk=blk, ncols=ncols: e.dma_start(out=wslot[s][:, 0:ncols], in_=wbf[blk, :, 0:ncols]),
               reads=[("wbf", blk)], writes=[("ws", s)], dma="w%d" % s)
            return s

        def norm_stats(chunks, scale, bias=EPS):
            pb = nbank()
            n = len(chunks)
            for c, (ap, rk) in enumerate(chunks):
                sqt = sq[c % 2]
                op("act", lambda e, ap=ap, sqt=sqt: e.activation(out=sqt[:, :], in_=ap, func=AF.Square),
                   reads=[rk], writes=[("sq", c % 2)])
                op("pe", lambda e, sqt=sqt, c=c, pb=pb: e.matmul(banks[pb][:, :], lhsT=onesb[:, :], rhs=sqt[:, :],
                                                                start=(c == 0), stop=(c == n - 1)),
                   reads=[("sq", c % 2), "onesb"], writes=[("ps", pb)])
            op("act", lambda e, pb=pb: e.activation(out=rt[:, :], in_=banks[pb][:, :], func=AF.Ln, bias=bias, scale=scale),
               reads=[("ps", pb)], writes=["rt"])
            ri = 0
            op("act", lambda e, ri=ri: e.activation(out=rstd[ri][:, :], in_=rt[:, :], func=AF.Exp, scale=-0.5), reads=["rt"], writes=[("rstd", ri)])
            return rstd[ri], ("rstd", ri)
        norm_stats.ctr = 0

        def prenorm(gcol, src=None):
            if src is None:
                src = [(h[:, c, :], ("h", c)) for c in range(8)]
            r, rk = norm_stats(src, 1.0 / D)
            for c in range(8):
                ap, k_ = src[c]
                op("dve", lambda e, c=c, r=r, ap=ap: e.scalar_tensor_tensor(out=u[:, c, :], in0=ap, scalar=smc(gcol + c),
                                                                             in1=r[:, :], op0=ALU.mult, op1=ALU.mult),
                   reads=[k_, rk, "smalls"], writes=[("u", c)])

        def postnorm_residual(gcol, weight, eps_mult=1.0, after=None):
            r, rk = norm_stats([(fbuf[:, c, :], ("F", "f", c)) for c in range(8)], 1.0 / (D * weight * weight), eps_mult * EPS / (weight * weight))
            for c in range(8):
                tmp = sg[c % 2]
                op("dve", lambda e, c=c, r=r, tmp=tmp: e.scalar_tensor_tensor(out=tmp[:, :], in0=fbuf[:, c, :], scalar=smc(gcol + c),
                                                                               in1=r[:, :], op0=ALU.mult, op1=ALU.mult),
                   reads=[("F", "f", c), rk, "smalls"], writes=[("sg", c % 2)])
                op("pool" if c % 2 == 0 else "dve", lambda e, c=c, tmp=tmp: e.tensor_tensor(out=h[:, c, :], in0=h[:, c, :], in1=tmp[:, :], op=ALU.add),
                   reads=[("sg", c % 2), ("h", c)], writes=[("h", c)])
                if after is not None:
                    after(c)

        def fm_proj(pb, src, nk, wap_fn, wreads, src_reads):
            for k in range(nk):
                op("pe", lambda e, k=k: e.matmul(banks[pb][:, :], lhsT=wap_fn(k), rhs=src[:, k, :], start=(k == 0), stop=(k == nk - 1)),
                   reads=list(wreads) + [src_reads(k)], writes=[("ps", pb)])

        def ffn_begin(which, gpre, b_gu, b_d, src=None):
            S_.label = "FFN%d.%d" % (which, tile_id[0])
            ctx = {"blks": [(b_gu + j, 4096) for j in range(11)] + [(b_d + dc, 2816) for dc in range(8)], "slots": {}, "which": which}

            def prefetch(i):
                if i < len(ctx["blks"]) and i not in ctx["slots"]:
                    ctx["slots"][i] = wload(*ctx["blks"][i])
            ctx["prefetch"] = prefetch
            prefetch(0)
            prefetch(1)
            prenorm(gpre, src)
            return ctx

        def ffn_mid(ctx):
            S_.label = "FFN%d.%d" % (ctx["which"], tile_id[0])
            prefetch, slots = ctx["prefetch"], ctx["slots"]
            S_.alias_switch("A")
            for j in range(11):
                prefetch(j + 2)
                s = slots[j]
                wv = wslot[s][:, :].rearrange("p (mm gu k c) -> p mm gu k c", mm=2, gu=2, k=8)
                for mm in range(2):
                    m = 2 * j + mm
                    pg = nbank()
                    fm_proj(pg, u, 8, lambda k, wv=wv, mm=mm: wv[:, mm, 0, k, :], [("ws", s)], lambda k: ("u", k))
                    pu = nbank()
                    fm_proj(pu, u, 8, lambda k, wv=wv, mm=mm: wv[:, mm, 1, k, :], [("ws", s)], lambda k: ("u", k))
                    sgt = sg[m % 2]
                    op("act", lambda e, pg=pg, sgt=sgt: e.activation(out=sgt[:, :], in_=banks[pg][:, :], func=AF.Silu),
                       reads=[("ps", pg)], writes=[("sg", m % 2)])
                    op("dve", lambda e, pu=pu, sgt=sgt, m=m: e.tensor_tensor(out=actv[:, m, :], in0=banks[pu][:, :], in1=sgt[:, :], op=ALU.mult),
                       reads=[("ps", pu), ("sg", m % 2)], writes=[("A", "act", m)])
                    if "hook" in ctx:
                        ctx["hook"](m)
            S_.alias_switch("F")
            for dc in range(8):
                prefetch(11 + dc + 2)
                s = slots[11 + dc]
                wv = wslot[s][:, 0:2816].rearrange("p (m c) -> p m c", c=128)
                pb = nbank()
                fm_proj(pb, actv, NM, lambda k, wv=wv: wv[:, k, :], [("ws", s)], lambda k: ("A", "act", k))
                op("dve", lambda e, pb=pb, dc=dc: e.tensor_copy(out=fbuf[:, dc, :], in_=banks[pb][:, :]),
                   reads=[("ps", pb)], writes=[("F", "f", dc)])

        def ffn_post(ctx, gpost, after=None):
            S_.label = "FFN%d.%d" % (ctx["which"], tile_id[0])
            postnorm_residual(gpost, 0.5, after=after)


        def tm_proj(pb, c0, ncol, tb, wap_fn, wreads):
            for k in range(8):
                op("pe", lambda e, k=k: e.matmul(banks[pb][:, c0:c0 + ncol], lhsT=u[:, k, tb * 128:(tb + 1) * 128], rhs=wap_fn(k),
                                                 start=(k == 0), stop=(k == 7)),
                   reads=list(wreads) + [("u", k)], writes=[("ps", pb)])

        tile_id = [0]

        def mixer(first):
            S_.label = "M1.%d" % tile_id[0]
            prenorm(SM_G + 16)
            S_.alias_switch("A")
            S_.alias_switch("F")
            S_.alias_switch("TT")
            if first:
                hz = halo[tile_id[0] % 2]
                op("pool", lambda e: e.memset(hz[:, :, :], 0.0), writes=[("halo%d" % (tile_id[0] % 2), c) for c in range(24)])
                op("pool", lambda e: e.memset(Hf[:, :, :], 0.0), writes=[("Hf", g) for g in range(4)])
                op("pool", lambda e: e.memset(Hb[:, :, :], 0.0), writes=[("Hb", g) for g in range(4)])
                op("pool", lambda e: e.memset(kT[:, :, 0:128], 0.0), writes=[("kT", kv) for kv in range(4)])
                op("pool", lambda e: e.memset(vaug[:, 0, :, 0:64], 0.0), writes=[("va", 0)])
            S_.label = "M2.%d" % tile_id[0]
            hin = halo[tile_id[0] % 2]
            hout = halo[(tile_id[0] + 1) % 2]
            hik = "halo%d" % (tile_id[0] % 2)
            hok = "halo%d" % ((tile_id[0] + 1) % 2)
            acc3 = [sg[0], sg[1], acc2]
            acck = [("sg", 0), ("sg", 1), "acc2"]
            m2 = {}

            def m2_A(c):
                j, cc = divmod(c, 4)
                if cc == 0:
                    m2["s"] = wload(B_XBC + j)
                sl = m2["s"]
                wv = wslot[sl][:, :].rearrange("p (k c) -> p k c", c=512)
                pb = nbank()
                fm_proj(pb, u, 8, lambda k: wv[:, k, cc * 128:(cc + 1) * 128], [("ws", sl)], lambda k: ("u", k))
                P = banks[pb]
                a_ = acc3[c % 3]
                xp = xpre[c % 2]
                xk = ("xpre", c % 2)
                op("act", lambda e: e.activation(out=xp[:, 3:T + 3], in_=P[:, :], func=AF.Copy), reads=[("ps", pb)], writes=[xk])
                op("act", lambda e: e.activation(out=a_[:, :], in_=P[:, :], func=AF.Identity, bias=smc(SM_CB + c), scale=smc(SM_CW + 3 * 24 + c)),
                   reads=[("ps", pb), "smalls"], writes=[acck[c % 3]])
                op("pool", lambda e: e.tensor_copy(out=xp[:, 0:3], in_=hin[:, c, :]), reads=[(hik, c), xk], writes=[xk])
                op("pool", lambda e: e.tensor_copy(out=hout[:, c, :], in_=xp[:, T:T + 3]), reads=[xk], writes=[(hok, c)])

            def m2_B(c):
                a_ = acc3[c % 3]
                ak = acck[c % 3]
                xp = xpre[c % 2]
                xk = ("xpre", c % 2)
                for sh in (1, 2, 3):
                    op("dve", lambda e: e.scalar_tensor_tensor(out=a_[:, :], in0=xp[:, 3 - sh:T + 3 - sh], scalar=smc(SM_CW + (3 - sh) * 24 + c),
                                                               in1=a_[:, :], op0=ALU.mult, op1=ALU.add),
                       reads=[xk, ak, "smalls"], writes=[ak])

            def m2_C(c):
                a_ = acc3[c % 3]
                op("act", lambda e: e.activation(out=xbc[:, c, :], in_=a_[:, :], func=AF.Silu),
                   reads=[acck[c % 3]], writes=[("A", "xbc", c)])

            for c in range(24 + 2):
                if 0 <= c - 2 < 24:
                    m2_C(c - 2)
                if c < 24:
                    m2_A(c)
                if 0 <= c - 1 < 24:
                    m2_B(c - 1)
            dump("xbc_%d" % tile_id[0], xbc, [("A", "xbc", c) for c in range(24)])
            S_.label = "M3.%d" % tile_id[0]
            s = wload(B_DT)
            wv = wslot[s][:, :].rearrange("p (k c) -> p k c", c=512)
            pb = nbank()
            for tb in range(4):
                tm_proj(pb, tb * 32, 32, tb, lambda k, wv=wv: wv[:, k, 0:32], [("ws", s)])
            dps = banks[pb][:, 0:128].rearrange("p (b h) -> p b h", h=32)
            dtb_bc = bcs[:, BC_DTB:BC_DTB + 32].unsqueeze(1).broadcast_to([128, 4, 32])
            a_bc4 = abc[:, :].unsqueeze(1).broadcast_to([128, 4, 32])
            op("dve", lambda e: e.tensor_tensor(out=dtmp[0][:, :, :], in0=dps, in1=dtb_bc, op=ALU.add),
               reads=[("ps", pb), "bcs"], writes=["dtmp0"])
            op("dve", lambda e: e.scalar_tensor_tensor(out=dtmp[1][:, :, :], in0=dtmp[0][:, :, :], scalar=-1.0, in1=dtmp[0][:, :, :],
                                                        op0=ALU.mult, op1=ALU.min),
               reads=["dtmp0"], writes=["dtmp1"])
            op("act", lambda e: e.activation(out=dtmp[1][:, :, :], in_=dtmp[1][:, :, :], func=AF.Exp),
               reads=["dtmp1"], writes=["dtmp1"])
            op("act", lambda e: e.activation(out=dtmp[2][:, :, :], in_=dtmp[1][:, :, :], func=AF.Ln, bias=1.0),
               reads=["dtmp1"], writes=["dtmp2"])
            op("dve", lambda e: e.scalar_tensor_tensor(out=dtt[:, :, :], in0=dtmp[0][:, :, :], scalar=0.0, in1=dtmp[2][:, :, :],
                                                        op0=ALU.max, op1=ALU.add),
               reads=["dtmp0", "dtmp2"], writes=["dtt"])
            op("dve", lambda e: e.tensor_tensor(out=dAf[:, :, :], in0=dtt[:, :, :], in1=a_bc4, op=ALU.mult),
               reads=["dtt", "abc"], writes=["dAf"])
            dump("dtt_%d" % tile_id[0], dtt[:, :, :], ["dtt"])
            S_.label = "M4.%d" % tile_id[0]
            for j in range(4):
                s = wload(B_Z + j)
                wv = wslot[s][:, :].rearrange("p (k c) -> p k c", c=512)
                for tb in range(4):
                    pb = nbank()
                    tm_proj(pb, 0, 512, tb, lambda k, wv=wv: wv[:, k, :], [("ws", s)])
                    op("act", lambda e, pb=pb, tb=tb, j=j: e.activation(out=szv[:, tb, j * 512:(j + 1) * 512], in_=banks[pb][:, :], func=AF.Silu),
                       reads=[("ps", pb)], writes=[("F", "sz", tb, j)])
            dump("sz_%d" % tile_id[0], szv, [("F", "sz", tb, j) for tb in range(4) for j in range(4)])
            S_.label = "M5.%d" % tile_id[0]
            D_bc = bcs[:, BC_DSK:BC_DSK + 32]

            def tsl(tb):
                return slice(tb * 128, (tb + 1) * 128)

            def chunk_pre(tb):
                cp = tb % 2
                ts_ = tsl(tb)
                xs_banks = []
                for half in range(2):
                    pb = nbank()
                    xs_banks.append(pb)
                    pbv = banks[pb][:, :].bitcast(BF16)
                    for cc in range(8):
                        c = half * 8 + cc
                        op("pe", lambda e: e.transpose(pbv[:, cc * 128:(cc + 1) * 128], xbc[:, c, ts_], identb[:, :]),
                           reads=[("A", "xbc", c), "identb"], writes=[("ps", pb)])
                    hs = slice(half * 16, half * 16 + 16)
                    pv3 = pbv.rearrange("p (h d) -> p h d", d=64)
                    op("dve", lambda e: e.tensor_tensor(
                        out=x_tok[cp][:, half * 1024:(half + 1) * 1024].rearrange("p (h d) -> p h d", d=64), in0=pv3,
                        in1=dtt[:, tb, hs].unsqueeze(2).broadcast_to([128, 16, 64]), op=ALU.mult),
                       reads=[("ps", pb), "dtt"], writes=[("TT", "x_tok", cp, half)])
                pb = nbank()
                pbv = banks[pb][:, :].bitcast(BF16)
                for g in range(4):
                    op("pe", lambda e: e.transpose(pbv[:, g * 128:(g + 1) * 128], xbc[:, 16 + g, ts_], identb[:, :]),
                       reads=[("A", "xbc", 16 + g), "identb"], writes=[("ps", pb)])
                op("act", lambda e: e.activation(out=B_tok[cp][:, :], in_=pbv[:, 0:512], func=AF.Copy),
                   reads=[("ps", pb)], writes=[("TT", "B_tok", cp)])
                pc = nbank()
                for i, (lt, lk) in enumerate(((cst[:, C_UI:C_UI + 128], "cst"), (cst[:, C_US:C_US + 128], "cst"), (onesf[:, :], "onesf"))):
                    op("pe", lambda e: e.matmul(banks[pc][:, i * 32:(i + 1) * 32], lhsT=lt, rhs=dAf[:, tb, :], start=True, stop=True),
                       reads=[lk, "dAf"], writes=[("ps", pc)])
                exps = dtmp[cp][:, 0:3, :]
                op("act", lambda e: e.activation(out=exps, in_=banks[pc][:, 0:96].rearrange("p (a h) -> p a h", h=32), func=AF.Exp),
                   reads=[("ps", pc)], writes=["dtmp%d" % cp])
                op("dve", lambda e: e.tensor_tensor(out=dtw[cp][:, :], in0=dtt[:, tb, :], in1=dtmp[cp][:, 1, :], op=ALU.mult),
                   reads=["dtt", "dtmp%d" % cp], writes=["dtw%d" % cp])
                for half in range(2):
                    pb = xs_banks[half]
                    pv3 = banks[pb][:, :].bitcast(BF16).rearrange("p (h d) -> p h d", d=64)
                    hs = slice(half * 16, half * 16 + 16)
                    op("dve", lambda e: e.tensor_tensor(
                        out=xw_all[cp][:, half * 1024:(half + 1) * 1024].rearrange("p (h d) -> p h d", d=64), in0=pv3,
                        in1=dtw[cp][:, hs].unsqueeze(2).broadcast_to([128, 16, 64]), op=ALU.mult),
                       reads=[("ps", pb), "dtw%d" % cp], writes=[("TT", "xw", cp, half)])

            def gi(n):
                tb, g = divmod(n, 4)
                return tb, g, tsl(tb), tb % 2, slice(8 * g, 8 * g + 8)

            def dve_dAU(n):
                tb, g, ts_, cp, h8 = gi(n)
                i2 = n % 2
                op("pool" if n % 2 == 0 else "dve", lambda e: e.tensor_tensor(
                    out=dAU[i2][:, :].rearrange("p (h i) -> p h i", i=128),
                    in0=dAf[:, tb, h8].unsqueeze(2).broadcast_to([128, 8, 128]),
                    in1=uinclb[:, :].unsqueeze(1).broadcast_to([128, 8, 128]), op=ALU.mult),
                   reads=["dAf", "uinclb"], writes=[("TT", "dAU", i2)])

            def dve_xw(n):
                tb, g, ts_, cp, h8 = gi(n)
                i2 = n % 2
                op("dve", lambda e: e.tensor_tensor(
                    out=xw[i2][:, :].rearrange("p (h d) -> p h d", d=64),
                    in0=x_tok[cp][:, g * 512:(g + 1) * 512].rearrange("p (h d) -> p h d", d=64),
                    in1=dtmp[cp][:, 1, h8].unsqueeze(2).broadcast_to([128, 8, 64]), op=ALU.mult),
                   reads=[("TT", "x_tok", cp, g // 2), "dtmp%d" % cp], writes=[("TT", "xw", i2)])

            def act_stats(n):
                tb, g, ts_, cp, h8 = gi(n)
                i3 = n % 3
                op("act", lambda e: e.activation(out=junkb, in_=yB[i3][:, :], func=AF.Square, scale=512.0 ** -0.5, accum_out=small1[:, g:g + 1]),
                   reads=[("TT", "yB", i3)], writes=[("TT", "junk"), ("small1", g)])
                op("act", lambda e: e.activation(out=small1[:, 4 + g:5 + g], in_=small1[:, g:g + 1], func=AF.Ln, bias=EPS),
                   reads=[("small1", g)], writes=[("small1", 4 + g)])
                op("act", lambda e: e.activation(out=small1[:, 8 + g:9 + g], in_=small1[:, 4 + g:5 + g], func=AF.Exp, scale=-0.5),
                   reads=[("small1", 4 + g)], writes=[("small1", 8 + g)])

            def dve_stt(n):
                tb, g, ts_, cp, h8 = gi(n)
                i3 = n % 3
                i2 = n % 2
                op("act", lambda e: e.activation(out=ynb[i2][:, :], in_=yB[i3][:, :], func=AF.Copy, scale=small1[:, 8 + g:9 + g]),
                   reads=[("TT", "yB", i3), ("small1", 8 + g)], writes=[("TT", "ynb", i2)])

            def pool_gate(n):
                tb, g, ts_, cp, h8 = gi(n)
                i3 = n % 3
                op("dve", lambda e: e.tensor_tensor(out=yB[i3][:, :], in0=yB[i3][:, :], in1=szv[:, tb, g * 512:(g + 1) * 512], op=ALU.mult),
                   reads=[("TT", "yB", i3), ("F", "sz", tb, g)], writes=[("TT", "yB", i3)])

            def act_Hb(n):
                tb, g, ts_, cp, h8 = gi(n)
                op("act", lambda e: e.activation(out=Hb[:, g, :], in_=Hf[:, g, :], func=AF.Copy),
                   reads=[("Hf", g)], writes=[("Hb", g)])

            seg_banks = {}

            def pe_seg(n):
                tb, g, ts_, cp, h8 = gi(n)
                i2 = n % 2
                seg_banks[n] = []
                for hh in range(2):
                    pg = nbank()
                    seg_banks[n].append(pg)
                    op("pe", lambda e: e.matmul(banks[pg][:, :], lhsT=ustrb[:, :], rhs=dAU[i2][:, hh * 512:(hh + 1) * 512], start=True, stop=True),
                       reads=["ustrb", ("TT", "dAU", i2)], writes=[("ps", pg)])

            def act_dec(n):
                i2 = n % 2
                for hh in range(2):
                    pg = seg_banks[n][hh]
                    op("act", lambda e: e.activation(out=dec[i2][:, hh * 512:(hh + 1) * 512], in_=banks[pg][:, :], func=AF.Exp),
                       reads=[("ps", pg)], writes=[("TT", "dec", i2, hh)])

            p2_banks = {}

            def pe_p2(n):
                tb, g, ts_, cp, h8 = gi(n)
                i2 = n % 2
                pyd = nbank()
                for cc in range(4):
                    c = 4 * g + cc
                    op("pe", lambda e: e.matmul(banks[pyd][:, cc * 128:(cc + 1) * 128], lhsT=xbc[:, c, ts_], rhs=Ddiag[:, c, :], start=True, stop=False),
                       reads=[("A", "xbc", c), "Ddiag"], writes=[("ps", pyd)])
                    for hh in range(2):
                        hl = 2 * cc + hh
                        hg = 8 * g + hl
                        op("pe", lambda e: e.matmul(banks[pyd][:, hl * 64:(hl + 1) * 64], lhsT=LT[i2][:, hl * 128:(hl + 1) * 128],
                                                    rhs=x_tok[cp][:, hg * 64:(hg + 1) * 64], start=False, stop=(hh == 1)),
                           reads=[("TT", "LT", i2), ("TT", "x_tok", cp, hg // 16)], writes=[("ps", pyd)])
                pyo = nbank()
                op("pe", lambda e: e.matmul(banks[pyo][:, :], lhsT=xbc[:, 20 + g, ts_], rhs=Hb[:, g, :], start=True, stop=True),
                   reads=[("A", "xbc", 20 + g), ("Hb", g)], writes=[("ps", pyo)])
                pst = nbank()
                op("pe", lambda e: e.matmul(banks[pst][:, :], lhsT=B_tok[cp][:, g * 128:(g + 1) * 128], rhs=xw_all[cp][:, g * 512:(g + 1) * 512], start=True, stop=True),
                   reads=[("TT", "B_tok", cp), ("TT", "xw", cp, g // 2)], writes=[("ps", pst)])
                p2_banks[n] = (pyd, pyo, pst)

            def pool_Hscale(n):
                tb, g, ts_, cp, h8 = gi(n)
                op("pool", lambda e: e.tensor_tensor(
                    out=Hf[:, g, :].rearrange("p (h d) -> p h d", d=64), in0=Hf[:, g, :].rearrange("p (h d) -> p h d", d=64),
                    in1=dtmp[cp][:, 2, h8].unsqueeze(2).broadcast_to([128, 8, 64]), op=ALU.mult),
                   reads=[("Hf", g), "dtmp%d" % cp], writes=[("Hf", g)])

            def pool_LT(n):
                i2 = n % 2
                op("pool", lambda e: e.tensor_tensor(
                    out=LT[i2][:, :].rearrange("p (h i) -> p h i", i=128),
                    in0=dec[i2][:, :].rearrange("p (h i) -> p h i", i=128),
                    in1=smg[i2][:, :].unsqueeze(1).broadcast_to([128, 8, 128]), op=ALU.mult),
                   reads=[("TT", "dec", i2, 0), ("TT", "dec", i2, 1), ("TT", "smg", i2)], writes=[("TT", "LT", i2)])

            tr_banks = {}

            def pe_tr(n):
                i2 = n % 2
                ptr = nbank()
                tr_banks[n] = ptr
                ptv = banks[ptr][:, :].bitcast(BF16)
                for cc in range(4):
                    op("pe", lambda e: e.transpose(ptv[:, cc * 128:(cc + 1) * 128], ynb[i2][:, cc * 128:(cc + 1) * 128], identb[:, :]),
                       reads=[("TT", "ynb", i2), "identb"], writes=[("ps", ptr)])

            def act_copy(n):
                tb, g, ts_, cp, h8 = gi(n)
                ptr = tr_banks[n]
                ptv = banks[ptr][:, :].bitcast(BF16)
                op("act", lambda e: e.activation(out=ynT[:, 4 * g:4 * g + 4, ts_], in_=ptv[:, 0:512].rearrange("p (c t) -> p c t", t=128), func=AF.Copy),
                   reads=[("ps", ptr)], writes=[("ynT", 4 * g + cc) for cc in range(4)])

            sc_banks = {}

            def pe_scores(n):
                tb, g, ts_, cp, h8 = gi(n)
                psc = nbank()
                sc_banks[n] = psc
                op("pe", lambda e: e.matmul(banks[psc][:, 0:128], lhsT=xbc[:, 16 + g, ts_], rhs=xbc[:, 20 + g, ts_], start=True, stop=True),
                   reads=[("A", "xbc", 16 + g), ("A", "xbc", 20 + g)], writes=[("ps", psc)])

            def dve_smg(n):
                i2 = n % 2
                psc = sc_banks[n]
                op("dve", lambda e: e.tensor_tensor(out=smg[i2][:, :], in0=banks[psc][:, 0:128], in1=cst[:, C_UI:C_UI + 128], op=ALU.mult),
                   reads=[("ps", psc), "cst"], writes=[("TT", "smg", i2)])

            def dve_combine(n):
                tb, g, ts_, cp, h8 = gi(n)
                i2 = n % 2
                i3 = n % 3
                pyd, pyo, pst = p2_banks[n]
                op("dve", lambda e: e.tensor_tensor(
                    out=yA[i2][:, :].rearrange("p (h d) -> p h d", d=64),
                    in0=banks[pyo][:, :].rearrange("p (h d) -> p h d", d=64),
                    in1=dtmp[cp][:, 0, h8].unsqueeze(2).broadcast_to([128, 8, 64]), op=ALU.mult),
                   reads=[("ps", pyo), "dtmp%d" % cp], writes=[("TT", "yA", i2)])
                op("dve", lambda e: e.tensor_tensor(out=yB[i3][:, :], in0=banks[pyd][:, :], in1=yA[i2][:, :], op=ALU.add),
                   reads=[("ps", pyd), ("TT", "yA", i2)], writes=[("TT", "yB", i3)])
                op("dve", lambda e: e.tensor_tensor(out=Hf[:, g, :], in0=banks[pst][:, :], in1=Hf[:, g, :], op=ALU.add),
                   reads=[("ps", pst), ("Hf", g)], writes=[("Hf", g)])

            kvp = {}

            def kv_piece(i):
                if i == 0:
                    kvp["sk"] = wload(B_K)
                if i == 4:
                    kvp["sv"] = wload(B_V)
                if i < 4:
                    kv = i
                    s_ = kvp["sk"]
                    wv = wslot[s_][:, :].rearrange("p (k c) -> p k c", c=512)
                    pb = nbank()
                    fm_proj(pb, u, 8, lambda k: wv[:, k, kv * 128:(kv + 1) * 128], [("ws", s_)], lambda k: ("u", k))
                    op("act", lambda e: e.activation(out=kT[:, kv, 128:128 + T], in_=banks[pb][:, :], func=AF.Copy),
                       reads=[("ps", pb)], writes=[("kT", kv)])
                else:
                    tb = i - 4
                    s_ = kvp["sv"]
                    wv = wslot[s_][:, :].rearrange("p (k c) -> p k c", c=512)
                    pb = nbank()
                    tm_proj(pb, 0, 256, tb, lambda k: wv[:, k, 0:256], [("ws", s_)])
                    op("act", lambda e: e.activation(out=vaug[:, 1 + tb, :, 0:64], in_=banks[pb][:, 0:256].rearrange("p (a d) -> p a d", d=64), func=AF.Copy),
                       reads=[("ps", pb)], writes=[("va", 1 + tb)])

            NG = 16

            def ok(m):
                return 0 <= m < NG

            chunk_pre(0)
            for it in range(NG + 5):
                if it >= 2 and (it - 2) % 4 == 0 and (it - 2) // 4 + 1 < 4:
                    chunk_pre((it - 2) // 4 + 1)
                if ok(it): dve_dAU(it)
                if ok(it - 4): act_stats(it - 4); dve_stt(it - 4)
                if ok(it - 3): pool_gate(it - 3); act_Hb(it - 3)
                if ok(it - 1): pe_seg(it - 1); act_dec(it - 1)
                if ok(it - 2): pe_p2(it - 2); pool_Hscale(it - 2)
                if ok(it - 1): pool_LT(it - 1)
                if ok(it - 5): pe_tr(it - 5)
                if ok(it): pe_scores(it); dve_smg(it)
                if ok(it - 2): dve_combine(it - 2)
                if ok(it - 5): act_copy(it - 5)
                if 4 <= it < 12:
                    kv_piece(it - 4)
            dump("ynT_%d" % tile_id[0], ynT[:, :, :], [("ynT", c) for c in range(16)])
            S_.label = "M6.%d" % tile_id[0]
            S_.alias_switch("F")
            m6 = {"sgs": None, "ssp": None}

            def m6_half(dc, half):
                if half == 0:
                    if dc % 2 == 0:
                        m6["ssp"] = wload(B_SP + dc // 2)
                    if dc % 4 == 0:
                        m6["sgs"] = wload(B_GS + dc // 4)
                    m6["py"] = nbank()
                ssp, sgs, py = m6["ssp"], m6["sgs"], m6["py"]
                wsp_v = wslot[ssp][:, :].rearrange("p (d k c) -> p d k c", d=2, k=16)
                wgs_v = wslot[sgs][:, :].rearrange("p (k c) -> p k c", c=512)
                ks = range(0, 12) if half == 0 else range(12, 16)
                for k in ks:
                    op("pe", lambda e: e.matmul(banks[py][:, :], lhsT=wsp_v[:, dc % 2, k, :], rhs=ynT[:, k, :], start=(k == 0), stop=(k == 15)),
                       reads=[("ws", ssp), ("ynT", k)], writes=[("ps", py)])
                if half == 1:
                    pg = nbank()
                    fm_proj(pg, u, 8, lambda k: wgs_v[:, k, (dc % 4) * 128:(dc % 4 + 1) * 128], [("ws", sgs)], lambda k: ("u", k))
                    sgt = sg[dc % 2]
                    op("act", lambda e: e.activation(out=sgt[:, :], in_=banks[pg][:, :], func=AF.Tanh, scale=0.5),
                       reads=[("ps", pg)], writes=[("sg", dc % 2)])
                    op("dve", lambda e: e.scalar_tensor_tensor(out=fbuf[:, dc, :], in0=sgt[:, :], scalar=1.0, in1=banks[py][:, :],
                                                               op0=ALU.add, op1=ALU.mult),
                       reads=[("ps", py), ("sg", dc % 2)], writes=[("F", "f", dc)])
            S_.label = "M7.%d" % tile_id[0]
            S_.alias_switch("A")
            S_.alias_switch("TT")
            for j in range(2):
                s = wload(B_Q + j)
                wv = wslot[s][:, :].rearrange("p (k c) -> p k c", c=512)
                for cc in range(4):
                    c = 4 * j + cc
                    pb = nbank()
                    fm_proj(pb, u, 8, lambda k, wv=wv, cc=cc: wv[:, k, cc * 128:(cc + 1) * 128], [("ws", s)], lambda k: ("u", k))
                    op("act", lambda e, pb=pb, c=c: e.activation(out=qT[:, c, :], in_=banks[pb][:, :], func=AF.Copy, scale=0.125),
                       reads=[("ps", pb)], writes=[("A", "qT", c)])
            at = {}

            def pe_scores_a(n):
                qb, kv = divmod(n, 4)
                qs = slice(qb * 128, (qb + 1) * 128)
                kcur = slice(128 + qb * 128, 128 + (qb + 1) * 128)
                kprev = slice(qb * 128, (qb + 1) * 128)
                at[n] = []
                for par in range(2):
                    pb = nbank()
                    at[n].append(pb)
                    base = par * 64
                    for jj in range(2):
                        hq = 4 * kv + 2 * jj + par
                        qc = hq // 2
                        op("pe", lambda e: e.matmul(banks[pb][:, jj * 256:jj * 256 + 128], lhsT=kT[base:base + 64, kv, kcur],
                                                    rhs=qT[base:base + 64, qc, qs], start=True, stop=True),
                           reads=[("kT", kv), ("A", "qT", qc)], writes=[("ps", pb)])
                        op("pe", lambda e: e.matmul(banks[pb][:, jj * 256 + 128:jj * 256 + 256], lhsT=kT[base:base + 64, kv, kprev],
                                                    rhs=qT[base:base + 64, qc, qs], start=True, stop=True),
                           reads=[("kT", kv), ("A", "qT", qc)], writes=[("ps", pb)])

            def act_exp_a(n):
                ip = n % 2
                for par in range(2):
                    pb = at[n][par]
                    pi = ip * 2 + par
                    op("act", lambda e: e.activation(out=Pf[pi][:, :], in_=banks[pb][:, :], func=AF.Exp),
                       reads=[("ps", pb)], writes=[("TT", "Pf", pi)])

            def pt_mult(n):
                qb, kv = divmod(n, 4)
                ip = n % 2
                for par, eng in ((1, "pool"), (0, "dve")):
                    pi = ip * 2 + par
                    h0 = 4 * kv + par
                    op(eng, lambda e: e.tensor_tensor(out=PTb[pi][:, :].rearrange("p (a c) -> p a c", c=256),
                                                      in0=Pf[pi][:, :].rearrange("p (a c) -> p a c", c=256),
                                                      in1=ebTb[:, h0:h0 + 3:2, :], op=ALU.mult),
                       reads=[("TT", "Pf", pi), "ebTb"], writes=[("TT", "PTb", pi)])

            pvb = {}

            def pe_pv(n):
                qb, kv = divmod(n, 4)
                ip = n % 2
                skip_prev = first and qb == 0
                po = nbank()
                pvb[n] = po
                for jl in range(4):
                    par = jl % 2
                    jj = jl // 2
                    pi = ip * 2 + par
                    ov = banks[po][:, jl * 65:(jl + 1) * 65]
                    if not skip_prev:
                        op("pe", lambda e: e.matmul(ov, lhsT=PTb[pi][:, jj * 256 + 128:jj * 256 + 256], rhs=vaug[:, qb, kv, :], start=True, stop=False),
                           reads=[("TT", "PTb", pi), ("va", qb)], writes=[("ps", po)])
                    op("pe", lambda e: e.matmul(ov, lhsT=PTb[pi][:, jj * 256:jj * 256 + 128], rhs=vaug[:, qb + 1, kv, :], start=skip_prev, stop=True),
                       reads=[("TT", "PTb", pi), ("va", qb + 1)], writes=[("ps", po)])

            def dve_norm(n):
                qb, kv = divmod(n, 4)
                ot = otok[qb % 2]
                po = pvb[n]
                o3 = banks[po][:, 0:260].rearrange("p (a d) -> p a d", d=65)
                den = small1[:, 12:16]
                op("dve", lambda e: e.tensor_tensor(out=den, in0=o3[:, :, 64], in1=esink[:, 4 * kv:4 * kv + 4], op=ALU.add),
                   reads=[("ps", po), "esink"], writes=[("small1", "den")])
                op("dve", lambda e: e.reciprocal(out=den, in_=den), reads=[("small1", "den")], writes=[("small1", "den")])
                op("dve", lambda e: e.tensor_tensor(
                    out=ot[:, kv * 256:(kv + 1) * 256].rearrange("p (a d) -> p a d", d=64), in0=o3[:, :, 0:64],
                    in1=den.unsqueeze(2).broadcast_to([128, 4, 64]), op=ALU.mult),
                   reads=[("ps", po), ("small1", "den")], writes=[("TT", "otok", qb % 2, kv)])

            def A3(qb):
                qs = slice(qb * 128, (qb + 1) * 128)
                ot = otok[qb % 2]
                ptr = nbank()
                ptv = banks[ptr][:, :].bitcast(BF16)
                for c in range(8):
                    op("pe", lambda e: e.transpose(ptv[:, c * 128:(c + 1) * 128], ot[:, c * 128:(c + 1) * 128], identb[:, :]),
                       reads=[("TT", "otok", qb % 2, c // 2), "identb"], writes=[("ps", ptr)])
                op("act", lambda e: e.activation(out=oT[:, :, qs], in_=ptv.rearrange("p (c t) -> p c t", t=128), func=AF.Copy),
                   reads=[("ps", ptr)], writes=[("A", "oT", c) for c in range(8)])

            for it in range(16 + 2):
                if it < 16:
                    pe_scores_a(it)
                    act_exp_a(it)
                if 0 <= it - 1 < 16:
                    pe_pv(it - 1)
                    dve_norm(it - 1)
                if it < 16:
                    S_.label = "M6.%d" % tile_id[0]
                    m6_half(it // 2, it % 2)
                    S_.label = "M7.%d" % tile_id[0]
                    pt_mult(it)
                if it - 2 >= 0 and (it - 2) % 4 == 3:
                    A3((it - 2) // 4)
            dump("oT_%d" % tile_id[0], oT, [("A", "oT", c) for c in range(8)])
            op("pool", lambda e: e.tensor_copy(out=kT[:, :, 0:128], in_=kT[:, :, T:T + 128]),
               reads=[("kT", kv) for kv in range(4)], writes=[("kT", kv) for kv in range(4)])
            op("pool", lambda e: e.tensor_copy(out=vaug[:, 0, :, 0:64], in_=vaug[:, 4, :, 0:64]),
               reads=[("va", 4)], writes=[("va", 0)])
            S_.label = "M8.%d" % tile_id[0]
            sga = None
            sap = None
            for dc in range(8):
                if dc % 4 == 0:
                    sap = wload(B_AP + dc // 4)
                    sga = wload(B_GA + dc // 4)
                wap_v = wslot[sap][:, :].rearrange("p (d k c) -> p d k c", d=4, k=8)
                wga_v = wslot[sga][:, :].rearrange("p (k c) -> p k c", c=512)
                pg = nbank()
                fm_proj(pg, u, 8, lambda k, wga_v=wga_v, dc=dc: wga_v[:, k, (dc % 4) * 128:(dc % 4 + 1) * 128], [("ws", sga)], lambda k: ("u", k))
                py = nbank()
                fm_proj(py, oT, 8, lambda k, wap_v=wap_v, dc=dc: wap_v[:, dc % 4, k, :], [("ws", sap)], lambda k: ("A", "oT", k))
                sgt = sg[dc % 2]
                op("act", lambda e, pg=pg, sgt=sgt: e.activation(out=sgt[:, :], in_=banks[pg][:, :], func=AF.Tanh, scale=0.5),
                   reads=[("ps", pg)], writes=[("sg", dc % 2)])
                op("dve", lambda e, py=py, sgt=sgt: e.scalar_tensor_tensor(out=sgt[:, :], in0=sgt[:, :], scalar=1.0, in1=banks[py][:, :],
                                                                          op0=ALU.add, op1=ALU.mult),
                   reads=[("ps", py), ("sg", dc % 2)], writes=[("sg", dc % 2)])
                op("pool", lambda e, sgt=sgt, dc=dc: e.tensor_tensor(out=mgb[:, dc, :], in0=fbuf[:, dc, :], in1=sgt[:, :], op=ALU.add),
                   reads=[("F", "f", dc), ("sg", dc % 2)], writes=[("A", "mgb", dc)])
            dump("mgb_%d" % tile_id[0], mgb, [("A", "mgb", c) for c in range(8)])
            S_.label = "M9.%d" % tile_id[0]
            for dc in range(8):
                if dc % 4 == 0:
                    so = wload(B_O + dc // 4)
                wo_v = wslot[so][:, :].rearrange("p (d k c) -> p d k c", d=4, k=8)
                pb = nbank()
                fm_proj(pb, mgb, 8, lambda k, wo_v=wo_v, dc=dc: wo_v[:, dc % 4, k, :], [("ws", so)], lambda k: ("A", "mgb", k))
                op("dve", lambda e, pb=pb, dc=dc: e.tensor_copy(out=fbuf[:, dc, :], in_=banks[pb][:, :]),
                   reads=[("ps", pb)], writes=[("F", "f", dc)])
            postnorm_residual(SM_G + 24, 1.0, eps_mult=4.0)

        def setup_late():
            S_.alias_switch("TT")
            wtmp = regT[:, 0:8192].bitcast(F32)
            wout = regT[:, 16384:20480]
            for j in range(4):
                op("sp", lambda e, j=j: e.dma_start(out=wtmp, in_=wall[B_SP + j]), writes=[("TT", "wtmp")], dma="c7")
                wt4 = wtmp.rearrange("p (d k c) -> p d k c", d=2, k=16)
                wo4 = wout.rearrange("p (d k c) -> p d k c", d=2, k=16)
                for kc in range(16):
                    op("dve", lambda e, kc=kc, wt4=wt4, wo4=wo4: e.tensor_scalar(out=wo4[:, :, kc, :], in0=wt4[:, :, kc, :], scalar1=smc(SM_NG + kc),
                                                                              scalar2=None, op0=ALU.mult),
                       reads=[("TT", "wtmp"), "smalls"], writes=[("TT", "wout")])
                op("sp", lambda e, j=j: e.dma_start(out=wbf[B_SP + j], in_=wout), reads=[("TT", "wout")], writes=[("wbf", B_SP + j)], dma="c8")
            S_.alias_switch("TT")
            pb = nbank()
            op("pe", lambda e, pb=pb: e.matmul(banks[pb][0:16, 0:128], lhsT=cst[0:32, C_TBL:C_TBL + 16],
                                               rhs=cst[0:32, C_BK:C_BK + 128], start=True, stop=True),
               reads=["cst"], writes=[("ps", pb)])
            op("act", lambda e, pb=pb: e.activation(out=ebt16[:, :], in_=banks[pb][0:16, 0:128], func=AF.Exp),
               reads=[("ps", pb)], writes=["ebt16"])
            op("sp", lambda e: e.dma_start(out=fpad[:, 0:128], in_=zero16[:, :]), reads=["zero16"], writes=["fpad0"], dma="c3")
            op("sp", lambda e: e.dma_start(out=fpad[:, 256:384], in_=zero16[:, :]), reads=["zero16"], writes=["fpad1"], dma="c4")
            op("sp", lambda e: e.dma_start(out=fpad[:, 128:256], in_=ebt16[:, :]), reads=["ebt16"], writes=["fpad2"], dma="c5")
            skew = bass.AP(fpad, 1, [[1, 128], [384, 16], [1, 256]])
            op("sp", lambda e: e.dma_start(out=ebTf, in_=skew), reads=["fpad0", "fpad1", "fpad2"],
               writes=[("TT", "ebTf")], dma="c6")
            for hp in range(8):
                pb = nbank()
                op("pe", lambda e, pb=pb, hp=hp: e.matmul(banks[pb][:, :], lhsT=cst[:, C_J:C_J + 128],
                                                          rhs=ebTf[:, 2 * hp:2 * hp + 2, :], start=True, stop=True),
                   reads=["cst", ("TT", "ebTf")], writes=[("ps", pb)])
                op("dve", lambda e, pb=pb, hp=hp: e.tensor_copy(out=ebTb[:, 2 * hp:2 * hp + 2, :],
                                                                in_=banks[pb][:, :].rearrange("p (h c) -> p h c", c=256)),
                   reads=[("ps", pb)], writes=["ebTb"])
            S_.alias_switch("TT")


        def h_store(si, t0, c):
            op("act", lambda e: e.dma_start(out=outT[si, c * 128:(c + 1) * 128, t0:t0 + T], in_=h[:, c, :]), reads=[("h", c)], dma="hs%d" % c)

        def h_load(si, t0, c):
            op("act", lambda e: e.dma_start(out=h[:, c, :], in_=xT[si, c * 128:(c + 1) * 128, t0:t0 + T]), writes=[("h", c)], dma="hl%d" % c)

        xstg = regT[:, 0:16384].bitcast(F32).rearrange("p (c t) -> p c t", t=T)

        def x_stage(si, t0, c):
            op("act", lambda e: e.dma_start(out=xstg[:, c, :], in_=xT[si, c * 128:(c + 1) * 128, t0:t0 + T]), writes=[("TT", "xs", c)], dma="hl%d" % c)

        tiles = [(si, it) for si in range(NSEQ) for it in range(NT)]
        ctx1 = None
        for idx, (si, it) in enumerate(tiles):
            t0 = it * T
            first = (it == 0)
            nxt = tiles[idx + 1] if idx + 1 < len(tiles) else None
            tile_id[0] = idx
            if idx == 0:
                for c in range(8):
                    h_load(si, t0, c)
            wstate["direct"] = (idx == 0)
            if ctx1 is None:
                ctx1 = ffn_begin(1, SM_G + 0, B_GU1, B_D1)
            if idx == 0:
                ctx1["hook"] = lambda m: cast_group(cast_at[m], m) if m in cast_at else None
            ffn_mid(ctx1)
            wstate["direct"] = False
            ffn_post(ctx1, SM_G + 8)
            ctx1 = None
            if idx == 0:
                S_.label = "SETUP"
                setup_late()
            if stage >= 2:
                mixer(first)
            if idx == 0:
                S_.alias_switch("TT")
            wstate["direct"] = (idx == 0)
            ctx2 = ffn_begin(2, SM_G + 32, B_GU2, B_D2)
            hoist = nxt is not None and idx != 0
            if hoist:
                S_.alias_switch("TT")
                for c in range(8):
                    x_stage(nxt[0], nxt[1] * T, c)
            ffn_mid(ctx2)
            wstate["direct"] = False
            if hoist:
                tile_id[0] = idx + 1
                ctx1 = ffn_begin(1, SM_G + 0, B_GU1, B_D1, src=[(xstg[:, c, :], ("TT", "xs", c)) for c in range(8)])
                tile_id[0] = idx

            def after(c, si=si, t0=t0, nxt=nxt, hoist=hoist):
                h_store(si, t0, c)
                if nxt is not None and not hoist:
                    h_load(nxt[0], nxt[1] * T, c)
            ffn_post(ctx2, SM_G + 40, after=after)
            if hoist:
                for c in range(8):
                    op("pool", lambda e, c=c: e.tensor_copy(out=h[:, c, :], in_=xstg[:, c, :]), reads=[("TT", "xs", c)], writes=[("h", c)])
        S_.final_wait("sp")
        S_.emit(nc)
    nc._sched_labels = S_.labels
    return nc


def _t5_bucket(d):
    import math
    max_exact = 16
    dd = np.maximum(d, 1).astype(np.float32)
    large = max_exact + (np.log(dd / max_exact) / math.log(128 / max_exact) * (32 - max_exact)).astype(np.int32)
    large = np.minimum(large, 31)
    return np.where(d < max_exact, d, large)


def _blk_kc(w, ncols_pad=512):
    n = w.shape[1]
    out = np.zeros((128, 8, ncols_pad), np.float32)
    out[:, :, :n] = w.reshape(8, 128, n).transpose(1, 0, 2)
    return out.reshape(128, 8 * ncols_pad)


def prep_shared(inp):
    f = np.float32
    wall = np.zeros((NBLK, 128, 4096), f)

    def put_gu(base, wg, wu):
        wg = wg.reshape(8, 128, NM, 128)
        wu = wu.reshape(8, 128, NM, 128)
        for j in range(11):
            blk = np.zeros((128, 2, 2, 8, 128), f)
            for mm in range(2):
                m = 2 * j + mm
                blk[:, mm, 0] = wg[:, :, m, :].transpose(1, 0, 2)
                blk[:, mm, 1] = wu[:, :, m, :].transpose(1, 0, 2)
            wall[base + j] = blk.reshape(128, 4096)

    def put_d(base, wd):
        wd = wd.reshape(NM, 128, 8, 128)
        for dc in range(8):
            wall[base + dc, :, :2816] = wd[:, :, dc, :].transpose(1, 0, 2).reshape(128, 2816)

    put_gu(B_GU1, inp["ffn1_w_gate"][0], inp["ffn1_w_up"][0])
    put_d(B_D1, inp["ffn1_w_down"][0])
    put_gu(B_GU2, inp["ffn2_w_gate"][0], inp["ffn2_w_up"][0])
    put_d(B_D2, inp["ffn2_w_down"][0])
    win = inp["w_in"][0]
    o = 0
    gs = win[:, o:o + 1024]; o += 1024
    ga = win[:, o:o + 1024]; o += 1024
    z = win[:, o:o + 2048]; o += 2048
    xbc = win[:, o:o + 3072]; o += 3072
    dt = win[:, o:o + 32]; o += 32
    q = win[:, o:o + 1024]; o += 1024
    k = win[:, o:o + 256]; o += 256
    v = win[:, o:o + 256]; o += 256
    for j in range(6):
        wall[B_XBC + j] = _blk_kc(xbc[:, j * 512:(j + 1) * 512])
    wall[B_DT] = _blk_kc(dt)
    for j in range(4):
        wall[B_Z + j] = _blk_kc(z[:, j * 512:(j + 1) * 512])
    for j in range(2):
        wall[B_GS + j] = _blk_kc(gs[:, j * 512:(j + 1) * 512])
        wall[B_GA + j] = _blk_kc(ga[:, j * 512:(j + 1) * 512])
        wall[B_Q + j] = _blk_kc(q[:, j * 512:(j + 1) * 512])
    kd = np.concatenate([np.concatenate([k[:, i * 64:(i + 1) * 64]] * 2, axis=1) for i in range(4)], axis=1)
    wall[B_K] = _blk_kc(kd)
    wall[B_V] = _blk_kc(v)
    wsp = inp["w_ssm_proj"][0].reshape(16, 128, 8, 128)
    for j in range(4):
        blk = np.zeros((128, 2, 16, 128), f)
        for d2 in range(2):
            blk[:, d2] = wsp[:, :, 2 * j + d2, :].transpose(1, 0, 2)
        wall[B_SP + j] = blk.reshape(128, 4096)
    for base, name in ((B_AP, "w_attn_proj"), (B_O, "w_out")):
        w = inp[name][0].reshape(8, 128, 8, 128)
        for j in range(2):
            blk = np.zeros((128, 4, 8, 128), f)
            for d4 in range(4):
                blk[:, d4] = w[:, :, 4 * j + d4, :].transpose(1, 0, 2)
            wall[base + j] = blk.reshape(128, 4096)

    smalls = np.zeros((128, NSM), f)
    for i, name in enumerate(("ffn1_pre_g", "ffn1_post_g", "mix_pre_g", "mix_post_g", "ffn2_pre_g", "ffn2_post_g")):
        smalls[:, SM_G + 8 * i:SM_G + 8 * i + 8] = inp[name][0].reshape(8, 128).T
    cw = inp["conv_w"][0]
    for kk in range(4):
        smalls[:, SM_CW + kk * 24:SM_CW + kk * 24 + 24] = cw[kk].reshape(24, 128).T
    smalls[:, SM_CB:SM_CB + 24] = inp["conv_b"][0].reshape(24, 128).T
    smalls[:, SM_DS:SM_DS + 16] = np.repeat(inp["d_skip"][0], 64).reshape(16, 128).T
    smalls[:, SM_NG:SM_NG + 16] = inp["ssm_norm_g"][0].reshape(16, 128).T

    bc = np.zeros((128, NBC), f)
    bc[:, BC_DTB:BC_DTB + 32] = inp["dt_bias"][0][None, :]
    bc[:, BC_ALOG:BC_ALOG + 32] = inp["a_log"][0][None, :]
    bc[:, BC_DSK:BC_DSK + 32] = inp["d_skip"][0][None, :]
    bc[:, BC_SINK:BC_SINK + 16] = inp["attn_sinks"][0][None, :]

    cst = np.zeros((128, NCST), f)
    idx = np.arange(128)
    cst[:, C_ID:C_ID + 128] = np.eye(128, dtype=f)
    cst[:, C_UI:C_UI + 128] = (idx[:, None] <= idx[None, :]).astype(f)
    cst[:, C_US:C_US + 128] = (idx[:, None] > idx[None, :]).astype(f)
    bk = _t5_bucket(idx)
    cst[0:32, C_BK:C_BK + 128] = (np.arange(32)[:, None] == bk[None, :]).astype(f)
    cst[0:32, C_TBL:C_TBL + 16] = inp["rel_bias_table"]
    cst[:, C_J:C_J + 128] = (idx[:, None] + idx[None, :] == 127).astype(f)
    return {"wall": wall, "smalls": smalls, "bcin": bc, "cst": cst}


_NC_CACHE = {}


def kernel(**inputs):
    inp = {k: np.asarray(v) for k, v in inputs.items()}
    x = inp["x"]
    B, S, _ = x.shape
    nseq = B // NCORES
    shared = prep_shared(inp)
    key = (nseq, S)
    if key not in _NC_CACHE:
        _NC_CACHE[key] = build(nseq, S)
    nc = _NC_CACHE[key]
    in_maps = []
    for c in range(NCORES):
        xT = np.ascontiguousarray(x[c * nseq:(c + 1) * nseq].transpose(0, 2, 1))
        m = dict(shared)
        m["xT"] = xT
        in_maps.append(m)
    res = run_bass_kernel_spmd(nc, in_maps, core_ids=list(range(NCORES)))
    out = np.empty((B, S, D), np.float32)
    for c in range(NCORES):
        out[c * nseq:(c + 1) * nseq] = res.results[c]["outT"].transpose(0, 2, 1)
    return out
```
